# Optimizing a Trainium2 kernel written in Bass

```python
import jax, jax.numpy as jnp
from jax import lax
import numpy as np

D_MODEL = 2048
BATCH = 4
SEQ = 4096
DEPTH = 1

EPS = 1e-6
A_HEADS = 16
A_KV_HEADS = 4
A_HEAD_DIM = 64
A_WIDTH = A_HEADS * A_HEAD_DIM
A_KV_WIDTH = A_KV_HEADS * A_HEAD_DIM
WINDOW = 128
BLOCK = 128
ROT_DIM = A_HEAD_DIM // 4
ROPE_THETA = 500000.0
B_HEADS = 8
B_KEY_DIM = 128
B_VAL_DIM = 128
B_KEY_WIDTH = B_HEADS * B_KEY_DIM
B_WIDTH = B_HEADS * B_VAL_DIM
CHUNK = 64
FF_DIM = 5632
PLE_DIM = 256
SPLITS = (A_WIDTH, A_KV_WIDTH, A_KV_WIDTH, B_KEY_WIDTH, B_KEY_WIDTH, B_WIDTH, B_WIDTH, D_MODEL, D_MODEL)
IN_DIM = A_WIDTH + 2 * A_KV_WIDTH + 2 * B_KEY_WIDTH + 2 * B_WIDTH + 2 * D_MODEL

kernel_name = "hybrid_swa_sink_hgrn2_macaron_ple"


def rmsnorm(x, w):
    x32 = x.astype(jnp.float32)
    y = x32 * lax.rsqrt(jnp.mean(x32 * x32, axis=-1, keepdims=True) + EPS)
    return (y * w.astype(jnp.float32)).astype(x.dtype)


def swiglu(h, w_gate, w_up, w_down):
    return (jax.nn.silu(h @ w_gate) * (h @ w_up)) @ w_down


def partial_rope(t, positions):
    t32 = t.astype(jnp.float32)
    rot, rest = t32[..., :ROT_DIM], t32[..., ROT_DIM:]
    inv_freq = jnp.power(jnp.float32(ROPE_THETA), -jnp.arange(0, ROT_DIM, 2, dtype=jnp.float32) / ROT_DIM)
    ang = positions.astype(jnp.float32)[..., None] * inv_freq
    cos, sin = jnp.cos(ang)[:, :, None, :], jnp.sin(ang)[:, :, None, :]
    x1, x2 = rot[..., :ROT_DIM // 2], rot[..., ROT_DIM // 2:]
    out = jnp.concatenate([x1 * cos - x2 * sin, x2 * cos + x1 * sin, rest], axis=-1)
    return out.astype(t.dtype)


def sliding_window_attention(q, k, v, sinks):
    b, s = q.shape[0], q.shape[1]
    nb = s // BLOCK
    g = A_HEADS // A_KV_HEADS
    qb = q.reshape(b, nb, BLOCK, A_KV_HEADS, g, A_HEAD_DIM)
    kb = k.reshape(b, nb, BLOCK, A_KV_HEADS, A_HEAD_DIM)
    vb = v.reshape(b, nb, BLOCK, A_KV_HEADS, A_HEAD_DIM)

    def with_prev(t):
        prev = jnp.pad(t[:, :-1], ((0, 0), (1, 0), (0, 0), (0, 0), (0, 0)))
        return jnp.concatenate([prev, t], axis=2)

    kk, vv = with_prev(kb), with_prev(vb)
    scale = A_HEAD_DIM ** -0.5
    scores = jnp.einsum('bnqhgd,bnkhd->bnhgqk', qb, kk).astype(jnp.float32) * scale
    qi = jnp.arange(BLOCK)[:, None]
    kj = jnp.arange(2 * BLOCK)[None, :]
    dist = qi + BLOCK - kj
    blk = jnp.arange(nb)[:, None, None]
    allowed = (dist >= 0) & (dist < WINDOW) & ((blk > 0) | (kj >= BLOCK))
    scores = jnp.where(allowed[None, :, None, None], scores, -jnp.inf)
    sink = sinks.astype(jnp.float32).reshape(A_KV_HEADS, g)[None, None, :, :, None, None]
    m = jnp.maximum(jnp.max(scores, axis=-1, keepdims=True), sink)
    e = jnp.exp(scores - m)
    denom = jnp.sum(e, axis=-1, keepdims=True) + jnp.exp(sink - m)
    probs = (e / denom).astype(v.dtype)
    o = jnp.einsum('bnhgqk,bnkhd->bnqhgd', probs, vv)
    return o.reshape(b, s, A_WIDTH)


def hgrn2_chunk_step(state, inp):
    q, k, v, log_f = inp
    cum = jnp.cumsum(log_f, axis=2)
    causal = jnp.tril(jnp.ones((CHUNK, CHUNK), dtype=bool))
    diff = cum[:, :, :, None, :] - cum[:, :, None, :, :]
    decay = jnp.exp(jnp.where(causal[None, None, :, :, None], diff, -jnp.inf))
    scores = jnp.einsum('bhtd,bhsd,bhtsd->bhts', q, k, decay)
    o = scores @ v + jnp.einsum('bhtd,bhde->bhte', q * jnp.exp(cum), state)
    last = cum[:, :, -1:, :]
    new_state = jnp.exp(last[:, :, 0, :])[..., None] * state + jnp.einsum('bhsd,bhse->bhde', k * jnp.exp(last - cum), v)
    return new_state, o


def hgrn2(q_pre, f_pre, i_in, lb):
    b, s = q_pre.shape[0], q_pre.shape[1]
    nc = s // CHUNK
    f32 = f_pre.astype(jnp.float32)
    q = jax.nn.silu(q_pre.astype(jnp.float32))
    log_f = jnp.logaddexp(jnp.log(lb), jnp.log1p(-lb) + jax.nn.log_sigmoid(f32))
    k = (1.0 - lb) * jax.nn.sigmoid(-f32)
    v = i_in.astype(jnp.float32)

    def to_chunks(t):
        return t.reshape(b, nc, CHUNK, B_HEADS, t.shape[-1]).transpose(1, 0, 3, 2, 4)

    state0 = jnp.zeros((b, B_HEADS, B_KEY_DIM, B_VAL_DIM), jnp.float32)
    _, o = lax.scan(hgrn2_chunk_step, state0, (to_chunks(q), to_chunks(k), to_chunks(v), to_chunks(log_f)))
    return o.transpose(1, 0, 3, 2, 4).reshape(b, s, B_HEADS, B_VAL_DIM)


def setup_inputs(seed: int = 0) -> dict:
    key = jax.random.key(seed)
    ks = jax.random.split(key, 24)

    def nrm(k, shape, fan_in):
        return jax.random.normal(k, shape, jnp.float32) * fan_in ** -0.5

    def gain(k, shape):
        return 1.0 + 0.02 * jax.random.normal(k, shape, jnp.float32)

    return {
        "x": jax.random.normal(ks[0], (BATCH, SEQ, D_MODEL), jnp.float32),
        "p": jax.random.normal(ks[1], (DEPTH, BATCH, SEQ, PLE_DIM), jnp.float32),
        "positions": jnp.broadcast_to(jnp.arange(SEQ, dtype=jnp.int32), (BATCH, SEQ)),
        "ffn1_norm": gain(ks[2], (DEPTH, D_MODEL)),
        "ffn1_w_gate": nrm(ks[3], (DEPTH, D_MODEL, FF_DIM), D_MODEL),
        "ffn1_w_up": nrm(ks[4], (DEPTH, D_MODEL, FF_DIM), D_MODEL),
        "ffn1_w_down": nrm(ks[5], (DEPTH, FF_DIM, D_MODEL), FF_DIM),
        "mix_norm": gain(ks[6], (DEPTH, D_MODEL)),
        "w_in": nrm(ks[7], (DEPTH, D_MODEL, IN_DIM), D_MODEL),
        "attn_sinks": 0.5 * jax.random.normal(ks[8], (DEPTH, A_HEADS), jnp.float32),
        "hgrn_lower_bound": 0.1 * jax.random.normal(ks[9], (DEPTH + 1, B_KEY_WIDTH), jnp.float32),
        "hgrn_norm": gain(ks[10], (DEPTH, B_WIDTH)),
        "w_up_a": nrm(ks[11], (DEPTH, A_WIDTH, D_MODEL), A_WIDTH),
        "w_up_b": nrm(ks[12], (DEPTH, B_WIDTH, D_MODEL), B_WIDTH),
        "w_out": nrm(ks[13], (DEPTH, D_MODEL, D_MODEL), D_MODEL),
        "ffn2_norm": gain(ks[14], (DEPTH, D_MODEL)),
        "ffn2_w_gate": nrm(ks[15], (DEPTH, D_MODEL, FF_DIM), D_MODEL),
        "ffn2_w_up": nrm(ks[16], (DEPTH, D_MODEL, FF_DIM), D_MODEL),
        "ffn2_w_down": nrm(ks[17], (DEPTH, FF_DIM, D_MODEL), FF_DIM),
        "ple_norm": gain(ks[18], (DEPTH, D_MODEL)),
        "ple_w_gate": nrm(ks[19], (DEPTH, D_MODEL, D_MODEL), D_MODEL),
        "ple_w_proj": nrm(ks[20], (DEPTH, PLE_DIM, D_MODEL), PLE_DIM),
        "final_norm": gain(ks[21], (D_MODEL,)),
    }


def reference(x, p, positions, ffn1_norm, ffn1_w_gate, ffn1_w_up, ffn1_w_down, mix_norm, w_in,
              attn_sinks, hgrn_lower_bound, hgrn_norm, w_up_a, w_up_b, w_out, ffn2_norm,
              ffn2_w_gate, ffn2_w_up, ffn2_w_down, ple_norm, ple_w_gate, ple_w_proj, final_norm):
    b, s = x.shape[0], x.shape[1]
    offsets = np.cumsum(SPLITS)[:-1].tolist()
    lb_all = jnp.cumsum(jax.nn.softmax(hgrn_lower_bound.astype(jnp.float32), axis=0), axis=0)
    for l in range(DEPTH):
        h = rmsnorm(x, ffn1_norm[l])
        x = x + 0.5 * swiglu(h, ffn1_w_gate[l], ffn1_w_up[l], ffn1_w_down[l])

        h = rmsnorm(x, mix_norm[l])
        proj = h @ w_in[l]
        q_a, k_a, v_a, q_b, f_b, i_b, og_b, gate_a, gate_b = jnp.split(proj, offsets, axis=-1)

        q_a = partial_rope(q_a.reshape(b, s, A_HEADS, A_HEAD_DIM), positions)
        k_a = partial_rope(k_a.reshape(b, s, A_KV_HEADS, A_HEAD_DIM), positions)
        v_a = v_a.reshape(b, s, A_KV_HEADS, A_HEAD_DIM)
        out_a = sliding_window_attention(q_a, k_a, v_a, attn_sinks[l])

        lb = lb_all[l].reshape(B_HEADS, B_KEY_DIM)
        o_b = hgrn2(q_b.reshape(b, s, B_HEADS, B_KEY_DIM), f_b.reshape(b, s, B_HEADS, B_KEY_DIM),
                    i_b.reshape(b, s, B_HEADS, B_VAL_DIM), lb)
        o_b = rmsnorm(o_b, hgrn_norm[l].reshape(B_HEADS, B_VAL_DIM)).astype(x.dtype)
        out_b = (o_b * jax.nn.silu(og_b.reshape(b, s, B_HEADS, B_VAL_DIM))).reshape(b, s, B_WIDTH)

        merged = jax.nn.sigmoid(gate_a) * (out_a @ w_up_a[l]) + jax.nn.sigmoid(gate_b) * (out_b @ w_up_b[l])
        x = x + merged @ w_out[l]

        h = rmsnorm(x, ffn2_norm[l])
        x = x + 0.5 * swiglu(h, ffn2_w_gate[l], ffn2_w_up[l], ffn2_w_down[l])

        g = jax.nn.sigmoid(rmsnorm(x, ple_norm[l]) @ ple_w_gate[l])
        x = x + g * (p[l].astype(x.dtype) @ ple_w_proj[l])
    return rmsnorm(x, final_norm)
```

```python
from contextlib import ExitStack
import numpy as np
import ml_dtypes
import concourse.bass as bass
import concourse.mybir as mybir
from concourse.bass_utils import run_bass_kernel_spmd

F32 = mybir.dt.float32
BF16 = mybir.dt.bfloat16
I32 = mybir.dt.int32
AF = mybir.ActivationFunctionType
ALU = mybir.AluOpType
AX = mybir.AxisListType

NCORES = 8
D = 2048
FF = 5632
FSPLIT = [(0, 3072), (3072, 5632)]
FH = 3072
TOK = 2048
TB = 512
NBLK = TOK // TB
LB = 128
EPS = 1e-6
IN_DIM = 9728
PLE = 256
WSLOT = 8192
NWSLOT = 4


class Buf:
    __slots__ = ("name", "w", "r", "dsem", "dcnt")

    def __init__(self, name):
        self.name = name
        self.w = {}
        self.r = {}
        self.dsem = None
        self.dcnt = 0


class Sched:
    def __init__(self, nc):
        self.nc = nc
        self.E = {"pe": nc.tensor, "act": nc.scalar, "dve": nc.vector, "pool": nc.gpsimd, "sp": nc.sync}
        self.sem = {e: nc.alloc_semaphore("prog_" + e) for e in ("pe", "act", "dve", "pool")}
        self.cnt = {e: 0 for e in self.sem}
        self.seen = {e: {} for e in self.E}
        self.nsem = 0

    def _wait(self, e, key, tok):
        kind, s, v = tok
        if kind == "e" and s == e and e == "pe":
            return
        if self.seen[e].get(key, 0) >= v:
            return
        sem = self.sem[s] if kind == "e" else s
        self.E[e].wait_ge(sem, v)
        self.seen[e][key] = v

    def deps(self, e, reads, writes):
        for b in reads:
            for k, t in b.w.items():
                self._wait(e, k, t)
        for b in writes:
            for k, t in b.w.items():
                self._wait(e, k, t)
            for k, t in b.r.items():
                self._wait(e, k, t)

    @staticmethod
    def _rec(d, key, tok):
        old = d.get(key)
        if old is None or old[2] < tok[2]:
            d[key] = tok

    def fin(self, e, inst, reads, writes, sig=True):
        if sig:
            self.cnt[e] += 1
            inst.then_inc(self.sem[e], 1)
            idx = self.cnt[e]
        else:
            idx = self.cnt[e] + 1
        tok = ("e", e, idx)
        for b in reads:
            self._rec(b.r, e, tok)
        for b in writes:
            self._rec(b.w, e, tok)

    def op(self, e, reads, writes, mk):
        self.deps(e, reads, writes)
        inst = mk(self.E[e])
        self.fin(e, inst, reads, writes, True)

    def mm(self, out, lhsT, rhs, start, stop, reads, writes, sig, skip=False):
        self.deps("pe", reads, writes)
        inst = self.nc.tensor.matmul(out, lhsT, rhs, start=start, stop=stop, skip_group_check=skip)
        self.fin("pe", inst, reads, writes, sig)

    def tr(self, out, in_, ident, reads, writes, sig):
        self.deps("pe", reads, writes)
        inst = self.nc.tensor.transpose(out, in_, ident)
        self.fin("pe", inst, reads, writes, sig)

    def dma(self, q, out, in_, reads, writes, sembuf):
        self.deps(q, reads, writes)
        if sembuf.dsem is None:
            sembuf.dsem = self.nc.alloc_semaphore("d_" + sembuf.name)
            self.nsem += 1
        sembuf.dcnt += 1
        self.E[q].dma_start(out=out, in_=in_).then_inc(sembuf.dsem, 16)
        tok = ("d", sembuf.dsem, 16 * sembuf.dcnt)
        key = "d" + sembuf.name
        for b in reads:
            self._rec(b.r, key, tok)
        for b in writes:
            self._rec(b.w, key, tok)

    def wait_all(self, e, bufs):
        self.deps(e, [], bufs)


class T:
    def __init__(self, t, name):
        self.t = t
        self.buf = Buf(name)


class Ring:
    def __init__(self, items):
        self.items = items
        self.i = 0

    def next(self):
        it = self.items[self.i % len(self.items)]
        self.i += 1
        return it


W_OFF = dict(qa=0, ka=1024, va=1280, qb=1536, fb=2560, ib=3584, og=4608, ga=5632, gb=7680)
CW = 1408
TWO_PI = 6.283185307179586


def build(mode="full", ncores=NCORES):
    nc = bass.Bass("TRN2", target_bir_lowering=False)
    S = Sched(nc)
    uid = [0]

    def din(name, shape, dt=F32):
        return nc.dram_tensor(name, list(shape), dt, kind="ExternalInput").ap()

    def dscr(name, shape, dt=F32):
        kind = "ExternalOutput" if (mode.startswith("dbg") and name not in ("st_loc", "st_all")) else "Internal"
        return nc.dram_tensor(name, list(shape), dt, kind=kind).ap()

    x_in = din("x", [TOK + LB, D])
    p_in = din("p", [TOK, PLE])
    pos_in = din("pos", [1, TOK + LB], I32)
    cst_in = din("consts", [128, CW])
    nrm_in = din("norms", [128, 6, 16])
    sml_in = din("small", [128, 64])
    W = {k: din(k, shp) for k, shp in [
        ("ffn1_w_gate", [D, FF]), ("ffn1_w_up", [D, FF]), ("ffn1_w_down", [FF, D]), ("w_in", [D, IN_DIM]),
        ("w_up_a", [1024, D]), ("w_up_b", [1024, D]), ("w_out", [D, D]),
        ("ffn2_w_gate", [D, FF]), ("ffn2_w_up", [D, FF]), ("ffn2_w_down", [FF, D]),
        ("ple_w_gate", [D, D]), ("ple_w_proj", [PLE, D])]}
    w_in = W["w_in"]
    y_out = nc.dram_tensor("y", [TOK, D], F32, kind="ExternalOutput").ap()
    x1s = dscr("x1s", [128, 16, TOK]); x1s_b = [Buf("x1s%d" % i) for i in range(NBLK)]
    oas = dscr("oas", [128, 8, TOK], BF16); oas_b = [Buf("oas%d" % i) for i in range(NBLK)]
    obs = dscr("obs", [128, 8, TOK]); obs_b = [Buf("obs%d" % i) for i in range(NBLK)]
    qts = dscr("qts", [128, 8, TOK], BF16); qts_b = [Buf("qts%d" % i) for i in range(NBLK)]
    st_loc = dscr("st_loc", [128, 1024]); st_loc_b = Buf("st_loc")
    st_all = dscr("st_all", [ncores * 128, 1024]); st_all_b = Buf("st_all")
    ybuf = Buf("ydram")
    dbg_bufs = []

    def mk(alloc, name, shape, dt):
        uid[0] += 1
        nm = "%s_%d" % (name, uid[0])
        return T(alloc(nm, list(shape), dt), nm)

    def sb(name, shape, dt):
        return mk(nc.alloc_sbuf_tensor, name, shape, dt)

    def ps(name, shape, dt=F32):
        return mk(nc.alloc_psum_tensor, name, shape, dt)

    class Scope:
        def __init__(self):
            self.st = ExitStack()
            self.items = []

        def sb(self, name, shape, dt):
            uid[0] += 1
            nm = "%s_%d" % (name, uid[0])
            t = T(self.st.enter_context(nc.sbuf_tensor(nm, list(shape), dt)), nm)
            self.items.append(t)
            return t

        def close(self):
            barrier(self.items)
            self.st.close()

    cst = sb("cst", [128, CW], F32)
    nrm = sb("nrm", [128, 6, 16], F32)
    sml = sb("sml", [128, 64], F32)
    ident_f = cst.t[:, 0:128]
    maskB = cst.t[:, 128:384]
    maskA = cst.t[:, 384:640]
    prot = cst.t[:, 640:768]
    resetm = cst.t[:, 768:1280]
    bdmask = cst.t[:, 1280:1408]
    ident_b = sb("ident_b", [128, 128], BF16)
    ones_b = sb("ones_b", [128, 128], BF16)
    cc = sb("cc", [128, 64], F32)
    epsc = cc.t[:, 0:1]; pic = cc.t[:, 1:2]
    lbc = cc.t[:, 8:16]; omlb = cc.t[:, 16:24]; nomlb = cc.t[:, 24:32]; negsink = cc.t[:, 32:48]
    hgw = sml.t[:, 16:24]; sink = sml.t[:, 24:40]; sel = sml.t[:, 40:48]; invf = sml.t[:, 48:49]
    zeros_f = sb("zeros_f", [128, 512], F32)
    bar = sb("bar", [128, 1], F32)
    wslots = Ring([sb("wslot%d" % i, [128, WSLOT], BF16) for i in range(NWSLOT)])
    hT = sb("hT", [128, 16, TB + LB], BF16)
    sq = Ring([sb("sq%d" % i, [128, TB + LB], BF16) for i in range(2)])
    sg = Ring([sb("sg%d" % i, [128, TB + LB], F32) for i in range(2)])
    rstd = sb("rstd", [128, TB + LB], F32)
    kT = sb("kT", [128, 4, LB + TB], BF16)
    Vpad = sb("Vpad", [128, 5, 4, 2, 128], BF16)
    S_f = sb("S_f", [128, 8, 128], F32)
    S_bf = sb("S_bf", [128, 8, 128], BF16)
    carry = sb("carry", [128, 8], F32)
    accs = Ring([ps("acc%d" % i, [128, 1024]) for i in range(2)])
    aux = Ring([ps("aux%d" % i, [128, 512]) for i in range(3)])
    auxb = ps("auxb", [128, 1024], BF16)

    def barrier(items):
        for e in ("pe", "act", "pool"):
            if S.cnt[e] > 0:
                S._wait("dve", e, ("e", e, S.cnt[e]))
        for t in items:
            S.deps("dve", [], [t.buf])
        S.op("dve", [], [bar.buf], lambda e: e.memset(bar.t[:], 0.0))
        m = S.cnt["dve"]
        for e in ("pe", "act", "sp"):
            S._wait(e, "dve", ("e", "dve", m))

    def dump(name, src):
        shp = list(src.t.shape)
        o = nc.dram_tensor("dbg_" + name, shp, F32, kind="ExternalOutput").ap()
        b = Buf("dbg_" + name)
        idx = tuple(slice(None) for _ in shp)
        S.dma("pool", o[idx], src.t[idx], [src.buf], [b], b)
        dbg_bufs.append(b)

    S.dma("sp", cst.t[:], cst_in[:, :], [], [cst.buf], cst.buf)
    S.dma("sp", nrm.t[:], nrm_in[:, :, :], [], [nrm.buf], nrm.buf)
    S.dma("sp", sml.t[:], sml_in[:, :], [], [sml.buf], sml.buf)
    S.op("dve", [cst.buf], [ident_b.buf], lambda e: e.tensor_copy(ident_b.t[:], ident_f))
    S.op("dve", [], [ones_b.buf], lambda e: e.memset(ones_b.t[:], 1.0))
    S.op("dve", [], [cc.buf], lambda e: e.memset(cc.t[:, 0:1], EPS))
    S.op("dve", [], [cc.buf], lambda e: e.memset(cc.t[:, 1:2], float(np.pi)))
    S.op("dve", [], [zeros_f.buf], lambda e: e.memset(zeros_f.t[:], 0.0))
    S.op("dve", [], [Vpad.buf], lambda e: e.memset(Vpad.t[:], 0.0))
    S.op("dve", [], [S_f.buf], lambda e: e.memset(S_f.t[:], 0.0))
    S.op("dve", [], [S_bf.buf], lambda e: e.memset(S_bf.t[:], 0.0))
    S.op("dve", [], [carry.buf], lambda e: e.memset(carry.t[:], 0.0))
    S.op("dve", [sml.buf], [cc.buf], lambda e: e.tensor_tensor(out=lbc, in0=sml.t[:, 0:8], in1=sml.t[:, 8:16], op=ALU.subtract))
    S.op("act", [cc.buf], [cc.buf], lambda e: e.activation(out=lbc, in_=lbc, func=AF.Sigmoid))
    S.op("dve", [cc.buf], [cc.buf], lambda e: e.tensor_scalar(out=omlb, in0=lbc, scalar1=-1.0, scalar2=1.0, op0=ALU.mult, op1=ALU.add))
    S.op("dve", [cc.buf], [cc.buf], lambda e: e.tensor_scalar(out=nomlb, in0=lbc, scalar1=-1.0, scalar2=None, op0=ALU.add))
    S.op("dve", [sml.buf], [cc.buf], lambda e: e.tensor_scalar(out=negsink, in0=sink, scalar1=-1.0, scalar2=None, op0=ALU.mult))

    def splits_of(n):
        return [(0, 512)] + ([(512, n)] if n > 512 else [])

    def transpose_in(src_dram_rows, ncol_chunks, xt, dst, c0, dst_is_bf16=False):
        S.dma("sp", xt.t[:, 0:ncol_chunks * 128], src_dram_rows, [], [xt.buf], xt.buf)
        for g in range((ncol_chunks + 3) // 4):
            a = aux.next()
            k = min(4, ncol_chunks - g * 4)
            for i in range(k):
                dc = g * 4 + i
                S.tr(a.t[:, i * 128:(i + 1) * 128], xt.t[:, dc * 128:(dc + 1) * 128], ident_f,
                     [xt.buf, cst.buf], [a.buf], sig=(i == k - 1))
            src = a.t[:, 0:k * 128].rearrange("p (i t) -> p i t", i=k)
            d = dst.t[:, g * 4:g * 4 + k, c0:c0 + 128]
            if g % 2 == 0:
                S.op("act", [a.buf], [dst.buf], lambda e: e.activation(out=d, in_=src, func=AF.Copy))
            else:
                S.op("dve", [a.buf], [dst.buf], lambda e: e.tensor_copy(d, src))

    def rmsnorm_T(src, n, widx, dst):
        spl = splits_of(n)
        a = accs.next()
        for dc in range(16):
            s = sq.next()
            S.op("act", [src.buf], [s.buf], lambda e: e.activation(out=s.t[:, :n], in_=src.t[:, dc, :n], func=AF.Square))
            for (c0, c1) in spl:
                S.mm(a.t[:, c0:c1], ones_b.t[:], s.t[:, c0:c1], dc == 0, dc == 15, [ones_b.buf, s.buf], [a.buf], sig=True)
        S.op("act", [a.buf, cc.buf], [rstd.buf],
             lambda e: e.activation(out=rstd.t[:, :n], in_=a.t[:, :n], func=AF.Sqrt, scale=1.0 / D, bias=epsc))
        S.op("dve", [rstd.buf], [rstd.buf], lambda e: e.reciprocal(rstd.t[:, :n], rstd.t[:, :n]))
        for dc in range(16):
            S.op("dve", [src.buf, rstd.buf, nrm.buf], [dst.buf],
                 lambda e: e.scalar_tensor_tensor(out=dst.t[:, dc, :n], in0=src.t[:, dc, :n],
                                                  scalar=nrm.t[:, widx, dc:dc + 1], in1=rstd.t[:, :n],
                                                  op0=ALU.mult, op1=ALU.mult))

    def plain(w, col0, K, fgroup):
        KC = K // 128

        def load(g, sl):
            c = col0 + g * fgroup
            S.dma("pool", sl.t[:, 0:KC * fgroup].rearrange("p (kc f) -> p kc f", f=fgroup),
                  w[:, c:c + fgroup].rearrange("(kc p) f -> p kc f", p=128), [], [sl.buf], sl.buf)
        return load

    def gemm(srcs, ngroups, fgroup, n, epi):
        spl = splits_of(n)
        for g in range(ngroups):
            slots = []
            for (load, K, rhs, rbufs) in srcs:
                assert (K // 128) * fgroup <= WSLOT
                sl = wslots.next()
                load(g, sl)
                slots.append(sl)
            for fc in range(fgroup // 128):
                for wi, sl in enumerate(slots):
                    (load, K, rhs, rbufs) = srcs[wi]
                    KC = K // 128
                    a = accs.next()
                    for (c0, c1) in spl:
                        for kc in range(KC):
                            o = kc * fgroup + fc * 128
                            S.mm(a.t[:, c0:c1], sl.t[:, o:o + 128], rhs(kc, c0, c1), kc == 0, kc == KC - 1,
                                 [sl.buf] + rbufs, [a.buf], sig=(kc == KC - 1))
                    epi(wi, g * (fgroup // 128) + fc, a)

    def gemm_tm(w, col0, K, cols, cgroup, lhs, lbufs, tiles, epi):
        KC = K // 128
        for g in range(cols // cgroup):
            sl = wslots.next()
            plain(w, col0, K, cgroup)(g, sl)
            for tl in tiles:
                a = accs.next()
                for kc in range(KC):
                    S.mm(a.t[:, 0:cgroup], lhs(kc, tl), sl.t[:, kc * cgroup:(kc + 1) * cgroup], kc == 0, kc == KC - 1,
                         [sl.buf] + lbufs, [a.buf], sig=(kc == KC - 1))
                epi(tl, g, a)

    def ffn(xT, actT, n, widx, wg, wu, wd):
        rmsnorm_T(xT, n, widx, hT)
        for (fa, fb) in FSPLIT:
            cur = {}

            def epi_gu(wi, fc, a):
                if wi == 0:
                    s = sg.next()
                    cur["s"] = s
                    S.op("act", [a.buf], [s.buf], lambda e: e.activation(out=s.t[:, :n], in_=a.t[:, :n], func=AF.Silu))
                else:
                    s = cur["s"]
                    S.op("dve", [a.buf, s.buf], [actT.buf],
                         lambda e: e.tensor_tensor(out=actT.t[:, fc, :n], in0=a.t[:, :n], in1=s.t[:, :n], op=ALU.mult))

            rh = (lambda kc, c0, c1: hT.t[:, kc, c0:c1])
            gemm([(plain(wg, fa, D, 512), D, rh, [hT.buf]), (plain(wu, fa, D, 512), D, rh, [hT.buf])],
                 (fb - fa) // 512, 512, n, epi_gu)

            def epi_d(wi, fc, a):
                S.op("dve", [a.buf, xT.buf], [xT.buf],
                     lambda e: e.scalar_tensor_tensor(out=xT.t[:, fc, :n], in0=a.t[:, :n], scalar=0.5,
                                                      in1=xT.t[:, fc, :n], op0=ALU.mult, op1=ALU.add))

            gemm([(plain(wd[fa:fb, :], 0, fb - fa, 256), fb - fa, lambda kc, c0, c1: actT.t[:, kc, c0:c1], [actT.buf])],
                 D // 256, 256, n, epi_d)

    def store_out(blk, src, xt):
        for t in range(TB // 128):
            for g in range(4):
                a = aux.next()
                for i in range(4):
                    dc = g * 4 + i
                    S.tr(a.t[:, i * 128:(i + 1) * 128], src.t[:, dc, t * 128:(t + 1) * 128], ident_f,
                         [src.buf, cst.buf], [a.buf], sig=(i == 3))
                dst = xt.t[:, g * 512:(g + 1) * 512]
                if g % 2 == 0:
                    S.op("act", [a.buf], [xt.buf], lambda e: e.activation(out=dst, in_=a.t[:, :], func=AF.Copy))
                else:
                    S.op("dve", [a.buf], [xt.buf], lambda e: e.tensor_copy(dst, a.t[:, :]))
            r0 = blk * TB + t * 128
            S.dma("sp", y_out[r0:r0 + 128, :], xt.t[:], [xt.buf], [ybuf], xt.buf)

    def phase_a(blk):
        n = TB + LB if blk == 0 else TB
        sc = Scope()
        xT = sc.sb("xT", [128, 16, TB + LB], F32)
        xt = sc.sb("xtok", [128, D], F32)
        actT = sc.sb("actT", [128, FH // 128, TB + LB], BF16)
        tiles = [(blk * TB + t * 128, t * 128) for t in range(4)]
        if blk == 0:
            tiles.append((TOK, TB))
        for (r0, c0) in tiles:
            transpose_in(x_in[r0:r0 + 128, :], 16, xt, xT, c0)
        ffn(xT, actT, n, 0, W["ffn1_w_gate"], W["ffn1_w_up"], W["ffn1_w_down"])
        rmsnorm_T(xT, n, 1, hT)
        S.dma("sp", x1s[:, :, blk * TB:(blk + 1) * TB], xT.t[:, :, 0:TB], [xT.buf], [x1s_b[blk]], xT.buf)
        if mode == "dbgA" and blk == 0:
            dump("x1T", xT)
            dump("hT", hT)
        sc.close()
        if mode == "ffn_only":
            return
        mixer_a(blk, n)

    def mixer_a(blk, n):
        sc = Scope()
        posi = sc.sb("posi", [128, TB + LB], I32)
        cosT = sc.sb("cosT", [128, TB + LB], F32)
        sinT = sc.sb("sinT", [128, TB + LB], F32)
        S.dma("sp", posi.t[:, 0:TB], pos_in[0:1, blk * TB:(blk + 1) * TB].broadcast_to([128, TB]), [], [posi.buf], posi.buf)
        if blk == 0:
            S.dma("sp", posi.t[:, TB:n], pos_in[0:1, TOK:TOK + LB].broadcast_to([128, LB]), [], [posi.buf], posi.buf)
        S.op("dve", [posi.buf], [sinT.buf], lambda e: e.tensor_copy(sinT.t[:, :n], posi.t[:, :n]))
        S.op("dve", [sinT.buf, sml.buf], [sinT.buf], lambda e: e.tensor_scalar(out=sinT.t[:, :n], in0=sinT.t[:, :n], scalar1=invf, scalar2=None, op0=ALU.mult))
        S.op("dve", [sinT.buf], [cosT.buf], lambda e: e.tensor_scalar(out=cosT.t[:, :n], in0=sinT.t[:, :n], scalar1=float(1.0 / TWO_PI), scalar2=12582912.0, op0=ALU.mult, op1=ALU.add))
        S.op("dve", [cosT.buf], [cosT.buf], lambda e: e.tensor_scalar(out=cosT.t[:, :n], in0=cosT.t[:, :n], scalar1=-12582912.0, scalar2=None, op0=ALU.add))
        S.op("dve", [cosT.buf, sinT.buf], [sinT.buf], lambda e: e.scalar_tensor_tensor(out=sinT.t[:, :n], in0=cosT.t[:, :n], scalar=-6.28125, in1=sinT.t[:, :n], op0=ALU.mult, op1=ALU.add))
        S.op("dve", [cosT.buf, sinT.buf], [sinT.buf], lambda e: e.scalar_tensor_tensor(out=sinT.t[:, :n], in0=cosT.t[:, :n], scalar=-(TWO_PI - 6.28125), in1=sinT.t[:, :n], op0=ALU.mult, op1=ALU.add))
        S.op("dve", [sinT.buf], [sinT.buf], lambda e: e.tensor_scalar(out=sinT.t[:, :n], in0=sinT.t[:, :n], scalar1=-3.1415925, scalar2=3.1415925, op0=ALU.max, op1=ALU.min))
        S.op("act", [sinT.buf], [cosT.buf], lambda e: e.activation(out=cosT.t[:, :n], in_=sinT.t[:, :n], func=AF.Sin, scale=0.5))
        S.op("act", [sinT.buf], [sinT.buf], lambda e: e.activation(out=sinT.t[:, :n], in_=sinT.t[:, :n], func=AF.Sin))
        S.op("dve", [cosT.buf], [cosT.buf], lambda e: e.tensor_tensor(out=cosT.t[:, :n], in0=cosT.t[:, :n], in1=cosT.t[:, :n], op=ALU.mult))
        S.op("dve", [cosT.buf], [cosT.buf], lambda e: e.tensor_scalar(out=cosT.t[:, :n], in0=cosT.t[:, :n], scalar1=-2.0, scalar2=1.0, op0=ALU.mult, op1=ALU.add))
        qf = Ring([sc.sb("qf%d" % i, [128, TB + LB], F32) for i in range(2)])
        rt = Ring([sc.sb("rt%d" % i, [128, TB + LB], F32) for i in range(2)])

        def rope(a, nn, writes):
            f = qf.next(); r = rt.next()
            S.op("act", [a.buf], [f.buf], lambda e: e.activation(out=f.t[:, :nn], in_=a.t[:, :nn], func=AF.Copy))
            a2 = accs.next()
            for (c0, c1) in splits_of(nn):
                S.mm(a2.t[:, c0:c1], prot, f.t[:, c0:c1], True, True, [cst.buf, f.buf], [a2.buf], sig=True)
            S.op("dve", [a2.buf, sinT.buf], [r.buf], lambda e: e.tensor_tensor(out=r.t[:, :nn], in0=a2.t[:, :nn], in1=sinT.t[:, :nn], op=ALU.mult))
            S.op("dve", [f.buf, cosT.buf], [f.buf], lambda e: e.tensor_tensor(out=f.t[:, :nn], in0=f.t[:, :nn], in1=cosT.t[:, :nn], op=ALU.mult))
            return f, r

        rh = (lambda kc, c0, c1: hT.t[:, kc, c0:c1])
        def load_k(g, sl):
            for half in range(2):
                c = W_OFF["ka"] + g * 64
                S.dma("pool", sl.t[:, 0:16 * 128].rearrange("p (kc f) -> p kc f", f=128)[:, :, half * 64:(half + 1) * 64],
                      w_in[:, c:c + 64].rearrange("(kc p) f -> p kc f", p=128), [], [sl.buf], sl.buf)

        def epi_k(wi, g, a):
            f, r = rope(a, n, None)
            S.op("dve", [f.buf, r.buf], [kT.buf], lambda e: e.tensor_tensor(out=kT.t[:, g, LB:LB + TB], in0=f.t[:, 0:TB], in1=r.t[:, 0:TB], op=ALU.add))
            if n > TB:
                S.op("dve", [f.buf, r.buf], [kT.buf], lambda e: e.tensor_tensor(out=kT.t[:, g, 0:LB], in0=f.t[:, TB:n], in1=r.t[:, TB:n], op=ALU.add))

        gemm([(load_k, D, rh, [hT.buf])], 4, 128, n, epi_k)

        vtiles = [0, 1, 2, 3] + ([4] if blk == 0 else [])

        def epi_v(tl, g, a):
            vt = 0 if tl == 4 else tl + 1
            src = a.t[:, 0:256].rearrange("p (g d) -> p g d", g=4)
            S.op("act", [a.buf], [Vpad.buf], lambda e: e.activation(out=Vpad.t[:, vt, :, 0, 0:64], in_=src, func=AF.Copy))
            S.op("dve", [a.buf], [Vpad.buf], lambda e: e.tensor_copy(Vpad.t[:, vt, :, 1, 64:128], src))

        gemm_tm(w_in, W_OFF["va"], D, 256, 256, lambda kc, tl: hT.t[:, kc, tl * 128:(tl + 1) * 128], [hT.buf], vtiles, epi_v)

        qrot = Ring([sc.sb("qrot%d" % i, [128, TB], BF16) for i in range(4)])
        oaT = Ring([sc.sb("oaT%d" % i, [128, TB], BF16) for i in range(4)])
        ssb = Ring([sc.sb("ssb%d" % i, [128, 256], F32) for i in range(2)])
        eb = Ring([sc.sb("eb%d" % i, [128, 256], F32) for i in range(2)])
        pb = Ring([sc.sb("pb%d" % i, [128, 256], BF16) for i in range(2)])
        pt = Ring([sc.sb("pt%d" % i, [128, 256], BF16) for i in range(2)])
        sm = Ring([sc.sb("sm%d" % i, [128, 8], F32) for i in range(4)])
        for g in range(4):
            qc = {}

            def epi_q(wi, fc, a):
                f, r = rope(a, TB, None)
                q = qrot.next()
                qc[fc] = q
                S.op("dve", [f.buf, r.buf], [q.buf], lambda e: e.tensor_tensor(out=q.t[:, :], in0=f.t[:, 0:TB], in1=r.t[:, 0:TB], op=ALU.add))

            gemm([(plain(w_in, W_OFF["qa"] + g * 256, D, 256), D, rh, [hT.buf])], 1, 256, TB, epi_q)
            oc = [oaT.next(), oaT.next()]
            for j in range(4):
                msk = maskA if (blk == 0 and j == 0) else maskB
                for pair in range(2):
                    oacc = aux.next()
                    for hh in range(2):
                        h = 4 * g + 2 * pair + hh
                        base = hh * 64
                        q = qc[pair]
                        a = aux.next()
                        S.mm(a.t[:, 0:256], q.t[base:base + 64, j * 128:(j + 1) * 128], kT.t[base:base + 64, g, j * 128:j * 128 + 256],
                             True, True, [q.buf, kT.buf], [a.buf], sig=True)
                        s_ = ssb.next(); e_ = eb.next(); p_ = pb.next(); t_ = pt.next(); m_ = sm.next()
                        S.op("dve", [a.buf, cst.buf], [s_.buf], lambda e: e.tensor_tensor(out=s_.t[:, :], in0=a.t[:, 0:256], in1=msk, op=ALU.add))
                        S.op("dve", [s_.buf], [m_.buf], lambda e: e.reduce_max(out=m_.t[:, 0:1], in_=s_.t[:, :], axis=AX.X))
                        S.op("dve", [m_.buf, cc.buf], [m_.buf], lambda e: e.tensor_scalar(out=m_.t[:, 1:2], in0=m_.t[:, 0:1], scalar1=-0.125, scalar2=negsink[:, h:h + 1], op0=ALU.mult, op1=ALU.min))
                        S.op("act", [s_.buf, m_.buf], [e_.buf], lambda e: e.activation(out=e_.t[:, :], in_=s_.t[:, :], func=AF.Exp, scale=0.125, bias=m_.t[:, 1:2]))
                        S.op("act", [m_.buf, sml.buf], [m_.buf], lambda e: e.activation(out=m_.t[:, 3:4], in_=m_.t[:, 1:2], func=AF.Exp, scale=1.0, bias=sink[:, h:h + 1]))
                        S.op("dve", [e_.buf], [m_.buf], lambda e: e.reduce_sum(out=m_.t[:, 2:3], in_=e_.t[:, :], axis=AX.X))
                        S.op("dve", [m_.buf], [m_.buf], lambda e: e.tensor_tensor(out=m_.t[:, 4:5], in0=m_.t[:, 2:3], in1=m_.t[:, 3:4], op=ALU.add))
                        S.op("dve", [m_.buf], [m_.buf], lambda e: e.reciprocal(m_.t[:, 5:6], m_.t[:, 4:5]))
                        S.op("dve", [e_.buf, m_.buf], [p_.buf], lambda e: e.tensor_scalar(out=p_.t[:, :], in0=e_.t[:, :], scalar1=m_.t[:, 5:6], scalar2=None, op0=ALU.mult))
                        S.tr(auxb.t[:, 0:128], p_.t[:, 0:128], ident_b.t[:], [p_.buf, ident_b.buf], [auxb.buf], sig=False)
                        S.tr(auxb.t[:, 128:256], p_.t[:, 128:256], ident_b.t[:], [p_.buf, ident_b.buf], [auxb.buf], sig=True)
                        S.op("act", [auxb.buf], [t_.buf], lambda e: e.activation(out=t_.t[:, :], in_=auxb.t[:, 0:256], func=AF.Copy))
                        for kb in range(2):
                            S.mm(oacc.t[:, 0:128], Vpad.t[:, j + kb, g, hh, :], t_.t[:, kb * 128:(kb + 1) * 128],
                                 (hh == 0 and kb == 0), (hh == 1 and kb == 1), [Vpad.buf, t_.buf], [oacc.buf], sig=(kb == 1))
                    o = oc[pair]
                    S.op("act", [oacc.buf], [o.buf], lambda e: e.activation(out=o.t[:, j * 128:(j + 1) * 128], in_=oacc.t[:, 0:128], func=AF.Copy))
            for pair in range(2):
                S.dma("sp", oas[:, 2 * g + pair, blk * TB:(blk + 1) * TB], oc[pair].t[:, :], [oc[pair].buf], [oas_b[blk]], oc[pair].buf)
        S.op("dve", [kT.buf], [kT.buf], lambda e: e.tensor_copy(kT.t[:, :, 0:LB], kT.t[:, :, TB:TB + LB]))
        S.op("dve", [Vpad.buf], [Vpad.buf], lambda e: e.tensor_copy(Vpad.t[:, 0], Vpad.t[:, 4]))
        if mode == "dbgA" and blk == 0:
            dump("kT", kT)
        sc.close()
        if mode == "attn_only":
            return
        sc = Scope()

        vb = sc.sb("vb", [128, 4, 1024], BF16)

        def epi_ib(tl, g, a):
            d = vb.t[:, tl, g * 512:(g + 1) * 512]
            if tl % 2 == 0:
                S.op("act", [a.buf], [vb.buf], lambda e: e.activation(out=d, in_=a.t[:, 0:512], func=AF.Copy))
            else:
                S.op("dve", [a.buf], [vb.buf], lambda e: e.tensor_copy(d, a.t[:, 0:512]))

        gemm_tm(w_in, W_OFF["ib"], D, 1024, 512, lambda kc, tl: hT.t[:, kc, tl * 128:(tl + 1) * 128], [hT.buf], [0, 1, 2, 3], epi_ib)

        qtT = sc.sb("qtT", [128, 8, TB], BF16)
        ktT = sc.sb("ktT", [128, 8, TB], BF16)
        elast = sc.sb("elast", [128, 8, 8], F32)
        qs = Ring([sc.sb("qs%d" % i, [128, TB], F32) for i in range(2)])
        tsg = Ring([sc.sb("tsg%d" % i, [128, TB], F32) for i in range(2)])
        tfg = Ring([sc.sb("tfg%d" % i, [128, TB], F32) for i in range(2)])
        tcu = Ring([sc.sb("tcu%d" % i, [128, TB], F32) for i in range(2)])
        tct = Ring([sc.sb("tct%d" % i, [128, TB], F32) for i in range(1)])
        tex = Ring([sc.sb("tex%d" % i, [128, TB], F32) for i in range(2)])
        qto = Ring([sc.sb("qto%d" % i, [128, TB], BF16) for i in range(2)])
        cur = {}

        def epi_h(wi, hd, a):
            if wi == 0:
                q_ = qs.next(); cur["q"] = q_
                S.op("act", [a.buf], [q_.buf], lambda e: e.activation(out=q_.t[:, :], in_=a.t[:, 0:TB], func=AF.Silu))
                return
            q_ = cur["q"]
            sg_ = tsg.next(); fg_ = tfg.next(); cu_ = tcu.next(); ct_ = tct.next()
            S.op("act", [a.buf], [sg_.buf], lambda e: e.activation(out=sg_.t[:, :], in_=a.t[:, 0:TB], func=AF.Sigmoid))
            S.op("dve", [sg_.buf, cc.buf], [fg_.buf], lambda e: e.tensor_scalar(out=fg_.t[:, :], in0=sg_.t[:, :], scalar1=omlb[:, hd:hd + 1], scalar2=lbc[:, hd:hd + 1], op0=ALU.mult, op1=ALU.add))
            S.op("act", [fg_.buf], [fg_.buf], lambda e: e.activation(out=fg_.t[:, :], in_=fg_.t[:, :], func=AF.Ln))
            S.op("dve", [sg_.buf, cc.buf], [sg_.buf], lambda e: e.tensor_scalar(out=sg_.t[:, :], in0=sg_.t[:, :], scalar1=nomlb[:, hd:hd + 1], scalar2=omlb[:, hd:hd + 1], op0=ALU.mult, op1=ALU.add))
            S.op("dve", [fg_.buf, cst.buf], [cu_.buf], lambda e: e.tensor_tensor_scan(out=cu_.t[:, :], data0=resetm, data1=fg_.t[:, :], initial=0.0, op0=ALU.mult, op1=ALU.add))
            S.op("dve", [fg_.buf, zeros_f.buf, carry.buf], [ct_.buf], lambda e: e.tensor_tensor_scan(out=ct_.t[:, :], data0=zeros_f.t[:, :], data1=fg_.t[:, :], initial=carry.t[:, hd:hd + 1], op0=ALU.add, op1=ALU.add))
            S.op("dve", [ct_.buf], [carry.buf], lambda e: e.tensor_copy(carry.t[:, hd:hd + 1], ct_.t[:, TB - 1:TB]))
            x1 = tex.next()
            S.op("act", [cu_.buf], [x1.buf], lambda e: e.activation(out=x1.t[:, :], in_=cu_.t[:, :], func=AF.Exp))
            S.op("dve", [q_.buf, x1.buf], [qtT.buf], lambda e: e.tensor_tensor(out=qtT.t[:, hd, :], in0=q_.t[:, :], in1=x1.t[:, :], op=ALU.mult))
            x2 = tex.next()
            S.op("act", [cu_.buf], [x2.buf], lambda e: e.activation(out=x2.t[:, :], in_=cu_.t[:, :], func=AF.Exp, scale=-1.0))
            S.op("dve", [sg_.buf, x2.buf], [ktT.buf], lambda e: e.tensor_tensor(out=ktT.t[:, hd, :], in0=sg_.t[:, :], in1=x2.t[:, :], op=ALU.mult))
            x3 = tex.next(); qo = qto.next()
            S.op("act", [ct_.buf], [x3.buf], lambda e: e.activation(out=x3.t[:, :], in_=ct_.t[:, :], func=AF.Exp))
            S.op("dve", [q_.buf, x3.buf], [qo.buf], lambda e: e.tensor_tensor(out=qo.t[:, :], in0=q_.t[:, :], in1=x3.t[:, :], op=ALU.mult))
            S.dma("sp", qts[:, hd, blk * TB:(blk + 1) * TB], qo.t[:, :], [qo.buf], [qts_b[blk]], qo.buf)
            S.op("act", [cu_.buf], [elast.buf], lambda e: e.activation(out=elast.t[:, hd, :], in_=cu_.t[:, :].rearrange("p (c t) -> p c t", t=64)[:, :, 63], func=AF.Exp))

        gemm([(plain(w_in, W_OFF["qb"], D, 512), D, rh, [hT.buf]), (plain(w_in, W_OFF["fb"], D, 512), D, rh, [hT.buf])], 2, 512, TB, epi_h)
        if mode == "dbgA" and blk == 0:
            dump("qtT", qtT); dump("ktT", ktT); dump("elast", elast)

        ktok = Ring([sc.sb("ktok%d" % i, [128, 128], BF16) for i in range(8)])
        AT = Ring([sc.sb("AT%d" % i, [128, 128], BF16) for i in range(4)])
        stmp = sc.sb("stmp", [128, 8, 128], F32)
        ol = Ring([sc.sb("ol%d" % i, [128, 8, 128], F32) for i in range(2)])
        for pr in range(4):
            oac = accs.next()
            su = accs.next()
            kts = []
            c0 = pr * 128
            for hd in range(8):
                S.tr(auxb.t[:, hd * 128:(hd + 1) * 128], ktT.t[:, hd, c0:c0 + 128], ident_b.t[:], [ktT.buf, ident_b.buf], [auxb.buf], sig=True)
                kk = ktok.next(); kts.append(kk)
                S.op("act", [auxb.buf], [kk.buf], lambda e: e.activation(out=kk.t[:, :], in_=auxb.t[:, hd * 128:(hd + 1) * 128], func=AF.Copy))
                a = aux.next()
                S.mm(a.t[:, 0:128], ktT.t[:, hd, c0:c0 + 128], qtT.t[:, hd, c0:c0 + 128], True, True, [ktT.buf, qtT.buf], [a.buf], sig=True)
                at = AT.next()
                S.op("dve", [a.buf, cst.buf], [at.buf], lambda e: e.tensor_tensor(out=at.t[:, :], in0=a.t[:, 0:128], in1=bdmask, op=ALU.mult))
                o_ = oac.t[:, hd * 128:(hd + 1) * 128]
                S.mm(o_, vb.t[:, pr, hd * 128:(hd + 1) * 128], at.t[:, :], hd % 4 == 0, False, [vb.buf, at.buf], [oac.buf], sig=False, skip=True)
                S.mm(oac.t[:, hd * 128:hd * 128 + 64], S_bf.t[:, hd, :], qtT.t[:, hd, c0:c0 + 64], False, False, [S_bf.buf, qtT.buf], [oac.buf], sig=False, skip=True)
                S.mm(su.t[:, hd * 128:(hd + 1) * 128], kk.t[0:64, :], vb.t[0:64, pr, hd * 128:(hd + 1) * 128], True, True, [kk.buf, vb.buf], [su.buf], sig=True)
            S.op("dve", [su.buf, S_f.buf], [stmp.buf], lambda e: e.tensor_tensor(out=stmp.t[:, :, :], in0=su.t[:, :].rearrange("p (h e) -> p h e", h=8), in1=S_f.t[:, :, :], op=ALU.add))
            for hd in range(8):
                S.op("dve", [stmp.buf, elast.buf], [S_f.buf], lambda e: e.tensor_scalar(out=S_f.t[:, hd, :], in0=stmp.t[:, hd, :], scalar1=elast.t[:, hd, 2 * pr:2 * pr + 1], scalar2=None, op0=ALU.mult))
                S.op("act", [stmp.buf, elast.buf], [S_bf.buf], lambda e: e.activation(out=S_bf.t[:, hd, :], in_=stmp.t[:, hd, :], func=AF.Copy, scale=elast.t[:, hd, 2 * pr:2 * pr + 1]))
            su2 = su
            for hd in range(8):
                kk = kts[hd]
                S.mm(oac.t[:, hd * 128 + 64:hd * 128 + 128], S_bf.t[:, hd, :], qtT.t[:, hd, c0 + 64:c0 + 128], False, True, [S_bf.buf, qtT.buf], [oac.buf], sig=(hd == 7), skip=True)
                S.mm(su2.t[:, hd * 128:(hd + 1) * 128], kk.t[64:128, :], vb.t[64:128, pr, hd * 128:(hd + 1) * 128], True, True, [kk.buf, vb.buf], [su2.buf], sig=(hd == 7))
            S.op("dve", [su2.buf, S_f.buf], [stmp.buf], lambda e: e.tensor_tensor(out=stmp.t[:, :, :], in0=su2.t[:, :].rearrange("p (h e) -> p h e", h=8), in1=S_f.t[:, :, :], op=ALU.add))
            for hd in range(8):
                S.op("dve", [stmp.buf, elast.buf], [S_f.buf], lambda e: e.tensor_scalar(out=S_f.t[:, hd, :], in0=stmp.t[:, hd, :], scalar1=elast.t[:, hd, 2 * pr + 1:2 * pr + 2], scalar2=None, op0=ALU.mult))
                S.op("act", [stmp.buf, elast.buf], [S_bf.buf], lambda e: e.activation(out=S_bf.t[:, hd, :], in_=stmp.t[:, hd, :], func=AF.Copy, scale=elast.t[:, hd, 2 * pr + 1:2 * pr + 2]))
            o_sb = ol.next()
            S.op("act", [oac.buf], [o_sb.buf], lambda e: e.activation(out=o_sb.t[:, :, :], in_=oac.t[:, :].rearrange("p (h t) -> p h t", h=8), func=AF.Copy))
            S.dma("sp", obs[:, :, blk * TB + c0:blk * TB + c0 + 128], o_sb.t[:, :, :], [o_sb.buf], [obs_b[blk]], o_sb.buf)
        sc.close()

    def exchange():
        S.dma("sp", st_loc[:, :], S_f.t[:, :, :].rearrange("p h e -> p (h e)"), [S_f.buf], [st_loc_b], S_f.buf)
        S.deps("pool", [st_loc_b], [st_all_b])
        ccsem = nc.alloc_semaphore("ccsem")
        nc.gpsimd.collective_compute("AllGather", ALU.bypass, replica_groups=[list(range(ncores))],
                                     ins=[st_loc], outs=[st_all]).then_inc(ccsem)
        tok = ("d", ccsem, 1)
        st_all_b.w["cc"] = tok
        st_loc_b.r["cc"] = tok
        sc = Scope()
        sr = Ring([sc.sb("sr%d" % i, [128, 1024], F32) for i in range(2)])
        S.op("dve", [], [S_f.buf], lambda e: e.memset(S_f.t[:], 0.0))
        Sf2 = S_f.t[:, :, :].rearrange("p h e -> p (h e)")
        for r in range(ncores):
            t = sr.next()
            S.dma("sp", t.t[:, :], st_all[r * 128:(r + 1) * 128, :], [st_all_b], [t.buf], t.buf)
            S.op("dve", [t.buf, S_f.buf, sml.buf], [S_f.buf], lambda e: e.scalar_tensor_tensor(out=Sf2, in0=t.t[:, :], scalar=sel[:, r:r + 1], in1=Sf2, op0=ALU.mult, op1=ALU.add))
        S.op("dve", [S_f.buf], [S_bf.buf], lambda e: e.tensor_copy(S_bf.t[:], S_f.t[:]))
        if mode.startswith("dbg"):
            dump("S_in", S_f)
        sc.close()

    def phase_b(blk):
        n = TB
        cols = slice(blk * TB, (blk + 1) * TB)
        sc0 = Scope()
        xT = sc0.sb("xT", [128, 16, TB], F32)
        S.dma("sp", xT.t[:, :, :], x1s[:, :, cols], [x1s_b[blk]], [xT.buf], xT.buf)
        rmsnorm_T(xT, n, 1, hT)
        rh = (lambda kc, c0, c1: hT.t[:, kc, c0:c1])
        sc1 = Scope()
        outbT = sc1.sb("outbT", [128, 8, TB], BF16)
        sc2 = Scope()
        ob = sc2.sb("ob", [128, 8, TB], F32)
        qtot = sc2.sb("qtot", [128, 8, TB], BF16)
        S.dma("sp", ob.t[:, :, :], obs[:, :, cols], [obs_b[blk]], [ob.buf], ob.buf)
        S.dma("sp", qtot.t[:, :, :], qts[:, :, cols], [qts_b[blk]], [qtot.buf], qtot.buf)
        for hd in range(8):
            a = accs.next()
            S.mm(a.t[:, 0:TB], S_bf.t[:, hd, :], qtot.t[:, hd, :], True, True, [S_bf.buf, qtot.buf], [a.buf], sig=True)
            S.op("dve", [a.buf, ob.buf], [ob.buf], lambda e: e.tensor_tensor(out=ob.t[:, hd, :], in0=a.t[:, 0:TB], in1=ob.t[:, hd, :], op=ALU.add))
            s = sq.next()
            S.op("act", [ob.buf], [s.buf], lambda e: e.activation(out=s.t[:, :TB], in_=ob.t[:, hd, :], func=AF.Square))
            a2 = accs.next()
            S.mm(a2.t[:, 0:TB], ones_b.t[:], s.t[:, 0:TB], True, True, [ones_b.buf, s.buf], [a2.buf], sig=True)
            S.op("act", [a2.buf, cc.buf], [rstd.buf], lambda e: e.activation(out=rstd.t[:, :TB], in_=a2.t[:, :TB], func=AF.Sqrt, scale=1.0 / 128, bias=epsc))
            S.op("dve", [rstd.buf], [rstd.buf], lambda e: e.reciprocal(rstd.t[:, :TB], rstd.t[:, :TB]))
            S.op("dve", [ob.buf, rstd.buf, sml.buf], [ob.buf], lambda e: e.scalar_tensor_tensor(out=ob.t[:, hd, :], in0=ob.t[:, hd, :], scalar=hgw[:, hd:hd + 1], in1=rstd.t[:, :TB], op0=ALU.mult, op1=ALU.mult))

        def epi_og(wi, fc, a):
            s = sg.next()
            S.op("act", [a.buf], [s.buf], lambda e: e.activation(out=s.t[:, :TB], in_=a.t[:, :TB], func=AF.Silu))
            S.op("dve", [ob.buf, s.buf], [outbT.buf], lambda e: e.tensor_tensor(out=outbT.t[:, fc, :], in0=ob.t[:, fc, :], in1=s.t[:, :TB], op=ALU.mult))

        gemm([(plain(w_in, W_OFF["og"], D, 512), D, rh, [hT.buf])], 2, 512, n, epi_og)
        if mode.startswith("dbg") and blk == 0:
            dump("outbT", outbT)
        sc2.close()
        sc3 = Scope()
        oaT = sc3.sb("oaTb", [128, 8, TB], BF16)
        mT = sc3.sb("mT", [128, 16, TB], BF16)
        tA = Ring([sc3.sb("tA%d" % i, [128, TB], F32) for i in range(2)])
        tM = Ring([sc3.sb("tM%d" % i, [128, TB], F32) for i in range(2)])
        S.dma("sp", oaT.t[:, :, :], oas[:, :, cols], [oas_b[blk]], [oaT.buf], oaT.buf)
        cur = {}

        def epi_m(wi, fc, a):
            if wi == 0:
                t = tA.next(); cur["a"] = t
                S.op("act", [a.buf], [t.buf], lambda e: e.activation(out=t.t[:, :], in_=a.t[:, :TB], func=AF.Sigmoid))
            elif wi == 1:
                t = tM.next(); cur["m"] = t
                S.op("dve", [a.buf, cur["a"].buf], [t.buf], lambda e: e.tensor_tensor(out=t.t[:, :], in0=a.t[:, :TB], in1=cur["a"].t[:, :], op=ALU.mult))
            elif wi == 2:
                t = tA.next(); cur["b"] = t
                S.op("act", [a.buf], [t.buf], lambda e: e.activation(out=t.t[:, :], in_=a.t[:, :TB], func=AF.Sigmoid))
            else:
                t = cur["b"]
                S.op("dve", [a.buf, t.buf], [t.buf], lambda e: e.tensor_tensor(out=t.t[:, :], in0=a.t[:, :TB], in1=t.t[:, :], op=ALU.mult))
                S.op("dve", [t.buf, cur["m"].buf], [mT.buf], lambda e: e.tensor_tensor(out=mT.t[:, fc, :], in0=t.t[:, :], in1=cur["m"].t[:, :], op=ALU.add))

        gemm([(plain(w_in, W_OFF["ga"], D, 256), D, rh, [hT.buf]),
              (plain(W["w_up_a"], 0, 1024, 256), 1024, lambda kc, c0, c1: oaT.t[:, kc, c0:c1], [oaT.buf]),
              (plain(w_in, W_OFF["gb"], D, 256), D, rh, [hT.buf]),
              (plain(W["w_up_b"], 0, 1024, 256), 1024, lambda kc, c0, c1: outbT.t[:, kc, c0:c1], [outbT.buf])],
             D // 256, 256, n, epi_m)

        def epi_o(wi, fc, a):
            S.op("dve", [a.buf, xT.buf], [xT.buf], lambda e: e.tensor_tensor(out=xT.t[:, fc, :], in0=a.t[:, :TB], in1=xT.t[:, fc, :], op=ALU.add))

        gemm([(plain(W["w_out"], 0, D, 512), D, lambda kc, c0, c1: mT.t[:, kc, c0:c1], [mT.buf])], D // 512, 512, n, epi_o)
        if mode.startswith("dbg") and blk == 0:
            dump("x2T", xT)
        sc3.close()
        sc1.close()
        sc4 = Scope()
        actT = sc4.sb("actT2", [128, FH // 128, TB], BF16)
        ffn(xT, actT, n, 2, W["ffn2_w_gate"], W["ffn2_w_up"], W["ffn2_w_down"])
        sc4.close()
        sc5 = Scope()
        ptok = sc5.sb("ptok", [128, PLE], F32)
        pT = sc5.sb("pT", [128, 2, TB], BF16)
        for t in range(4):
            r0 = blk * TB + t * 128
            transpose_in(p_in[r0:r0 + 128, :], 2, ptok, pT, t * 128)
        rmsnorm_T(xT, n, 3, hT)
        cur2 = {}

        def epi_p(wi, fc, a):
            if wi == 0:
                s = sg.next(); cur2["s"] = s
                S.op("act", [a.buf], [s.buf], lambda e: e.activation(out=s.t[:, :TB], in_=a.t[:, :TB], func=AF.Sigmoid))
            else:
                s = cur2["s"]
                S.op("dve", [a.buf, s.buf], [s.buf], lambda e: e.tensor_tensor(out=s.t[:, :TB], in0=a.t[:, :TB], in1=s.t[:, :TB], op=ALU.mult))
                S.op("dve", [s.buf, xT.buf], [xT.buf], lambda e: e.tensor_tensor(out=xT.t[:, fc, :], in0=s.t[:, :TB], in1=xT.t[:, fc, :], op=ALU.add))

        gemm([(plain(W["ple_w_gate"], 0, D, 512), D, rh, [hT.buf]),
              (plain(W["ple_w_proj"], 0, PLE, 512), PLE, lambda kc, c0, c1: pT.t[:, kc, c0:c1], [pT.buf])],
             D // 512, 512, n, epi_p)
        sc5.close()
        rmsnorm_T(xT, n, 4, xT)
        sc6 = Scope()
        xt = sc6.sb("xtok_o", [128, D], F32)
        store_out(blk, xT, xt)
        sc6.close()
        sc0.close()

    nb = NBLK
    for blk in range(nb):
        phase_a(blk)
    exchange()
    for blk in range(nb):
        phase_b(blk)

    S.wait_all("sp", [ybuf])
    for b in dbg_bufs:
        S.wait_all("pool", [b])
    return nc


def _consts(first_of_seq):
    c = np.zeros((128, CW), np.float32)
    c[:, 0:128] = np.eye(128, dtype=np.float32)
    q = np.arange(128)[:, None]
    k = np.arange(256)[None, :]
    dist = q + 128 - k
    allowed = (dist >= 0) & (dist < 128)
    c[:, 128:384] = np.where(allowed, 0.0, -1e9)
    allowedA = allowed & (k >= 128) if first_of_seq else allowed
    c[:, 384:640] = np.where(allowedA, 0.0, -1e9)
    for m in range(128):
        dm = m % 64
        if dm < 8:
            c[m + 8, 640 + m] = -1.0
        elif dm < 16:
            c[m - 8, 640 + m] = 1.0
    t = np.arange(512)
    c[:, 768:1280] = (t % 64 != 0).astype(np.float32)[None, :]
    s = np.arange(128)[:, None]
    tt = np.arange(128)[None, :]
    c[:, 1280:1408] = ((s // 64 == tt // 64) & (s <= tt)).astype(np.float32)
    return c


def _norm_pack(*vs):
    out = np.zeros((128, 6, 16), np.float32)
    for i, v in enumerate(vs):
        out[:, i, :] = np.asarray(v, np.float32).reshape(16, 128).T
    return out


def _small(hgrn_lower_bound, hgrn_norm, attn_sinks, core, partner):
    s = np.zeros((128, 64), np.float32)
    lbp = np.asarray(hgrn_lower_bound, np.float32)
    s[:, 0:8] = lbp[0].reshape(8, 128).T
    s[:, 8:16] = lbp[1].reshape(8, 128).T
    s[:, 16:24] = np.asarray(hgrn_norm, np.float32).reshape(8, 128).T
    s[:, 24:40] = np.asarray(attn_sinks, np.float32).reshape(1, 16)
    if partner is not None:
        s[:, 40 + partner] = 1.0
    inv = np.power(np.float32(500000.0), -np.arange(0, 16, 2, dtype=np.float32) / np.float32(16)).astype(np.float32)
    for m in range(128):
        dm = m % 64
        if dm < 16:
            s[m, 48] = inv[dm % 8]
    return s


def make_in_maps(inputs, ncores=NCORES):
    x = np.asarray(inputs["x"], np.float32)
    B, SEQ, _ = x.shape
    xf = x.reshape(B * SEQ, D)
    pf = np.asarray(inputs["p"], np.float32).reshape(B * SEQ, PLE)
    posf = np.asarray(inputs["positions"]).astype(np.int32).reshape(B * SEQ)
    norms = _norm_pack(inputs["ffn1_norm"][0], inputs["mix_norm"][0], inputs["ffn2_norm"][0], inputs["ple_norm"][0], inputs["final_norm"])
    wts = {k: np.ascontiguousarray(np.asarray(inputs[k], np.float32)[0]) for k in
           ("ffn1_w_gate", "ffn1_w_up", "ffn1_w_down", "w_in", "w_up_a", "w_up_b", "w_out",
            "ffn2_w_gate", "ffn2_w_up", "ffn2_w_down", "ple_w_gate", "ple_w_proj")}
    in_maps = []
    for c in range(ncores):
        r0 = c * TOK
        first = (r0 % SEQ) == 0
        xc = np.zeros((TOK + LB, D), np.float32)
        xc[:TOK] = xf[r0:r0 + TOK]
        pc = np.zeros((1, TOK + LB), np.int32)
        pc[0, :TOK] = posf[r0:r0 + TOK]
        if not first:
            xc[TOK:] = xf[r0 - LB:r0]
            pc[0, TOK:] = posf[r0 - LB:r0]
        m = {"x": xc, "p": np.ascontiguousarray(pf[r0:r0 + TOK]), "pos": pc, "consts": _consts(first), "norms": norms,
             "small": _small(inputs["hgrn_lower_bound"], inputs["hgrn_norm"][0], inputs["attn_sinks"][0], c, None if first else c - 1)}
        m.update(wts)
        in_maps.append(m)
    return in_maps, (B, SEQ)


def kernel(**inputs):
    in_maps, (B, SEQ) = make_in_maps(inputs)
    nc = build("full")
    res = run_bass_kernel_spmd(nc, in_maps, core_ids=list(range(NCORES)))
    y = np.concatenate([r["y"] for r in res.results], axis=0)
    return y.reshape(B, SEQ, D).astype(np.float32)
```

```python
from contextlib import ExitStack
import numpy as np
import ml_dtypes
import concourse.bass as bass
import concourse.mybir as mybir
from concourse.bass_utils import run_bass_kernel_spmd

F32 = mybir.dt.float32
BF16 = mybir.dt.bfloat16
I32 = mybir.dt.int32
AF = mybir.ActivationFunctionType
ALU = mybir.AluOpType
AX = mybir.AxisListType

NCORES = 8
D = 2048
FF = 5632
FSPLIT = [(0, 3072), (3072, 5632)]
FH = 3072
TOK = 2048
TB = 512
NBLK = TOK // TB
LB = 128
EPS = 1e-6
IN_DIM = 9728
PLE = 256
WSLOT = 8192
NWSLOT = 4


class Buf:
    __slots__ = ("name", "w", "r", "dsem", "dcnt")

    def __init__(self, name):
        self.name = name
        self.w = {}
        self.r = {}
        self.dsem = None
        self.dcnt = 0


class Sched:
    def __init__(self, nc):
        self.nc = nc
        self.E = {"pe": nc.tensor, "act": nc.scalar, "dve": nc.vector, "pool": nc.gpsimd, "sp": nc.sync}
        self.sem = {e: nc.alloc_semaphore("prog_" + e) for e in ("pe", "act", "dve", "pool")}
        self.cnt = {e: 0 for e in self.sem}
        self.seen = {e: {} for e in self.E}
        self.nsem = 0
        self.sem_pool = []
        self.sem_cnt = {}

    def _wait(self, e, key, tok):
        kind, s, v = tok
        if kind == "e" and s == e and e == "pe":
            return
        if self.seen[e].get(key, 0) >= v:
            return
        sem = self.sem[s] if kind == "e" else s
        self.E[e].wait_ge(sem, v)
        self.seen[e][key] = v

    def deps(self, e, reads, writes):
        for b in reads:
            for k, t in b.w.items():
                self._wait(e, k, t)
        for b in writes:
            for k, t in b.w.items():
                self._wait(e, k, t)
            for k, t in b.r.items():
                self._wait(e, k, t)

    @staticmethod
    def _rec(d, key, tok):
        old = d.get(key)
        if old is None or old[2] < tok[2]:
            d[key] = tok

    def fin(self, e, inst, reads, writes, sig=True):
        if sig:
            self.cnt[e] += 1
            inst.then_inc(self.sem[e], 1)
            idx = self.cnt[e]
        else:
            idx = self.cnt[e] + 1
        tok = ("e", e, idx)
        for b in reads:
            self._rec(b.r, e, tok)
        for b in writes:
            self._rec(b.w, e, tok)

    def op(self, e, reads, writes, mk):
        self.deps(e, reads, writes)
        inst = mk(self.E[e])
        self.fin(e, inst, reads, writes, True)

    def mm(self, out, lhsT, rhs, start, stop, reads, writes, sig, skip=False):
        self.deps("pe", reads, writes)
        inst = self.nc.tensor.matmul(out, lhsT, rhs, start=start, stop=stop, skip_group_check=skip)
        self.fin("pe", inst, reads, writes, sig)

    def tr(self, out, in_, ident, reads, writes, sig):
        self.deps("pe", reads, writes)
        inst = self.nc.tensor.transpose(out, in_, ident)
        self.fin("pe", inst, reads, writes, sig)

    def dma(self, q, out, in_, reads, writes, sembuf):
        self.deps(q, reads, writes)
        if sembuf.dsem is None:
            if self.sem_pool:
                sembuf.dsem = self.sem_pool.pop()
            else:
                self.nsem += 1
                sembuf.dsem = (self.nc.alloc_semaphore("dsem%d" % self.nsem), self.nsem)
                self.sem_cnt[self.nsem] = 0
        sem, sid = sembuf.dsem
        self.sem_cnt[sid] += 1
        self.E[q].dma_start(out=out, in_=in_).then_inc(sem, 16)
        tok = ("d", sem, 16 * self.sem_cnt[sid])
        key = "dsem%d" % sid
        for b in reads:
            self._rec(b.r, key, tok)
        for b in writes:
            self._rec(b.w, key, tok)

    def wait_all(self, e, bufs):
        self.deps(e, [], bufs)


class T:
    def __init__(self, t, name):
        self.t = t
        self.buf = Buf(name)


class Ring:
    def __init__(self, items):
        self.items = items
        self.i = 0

    def next(self):
        it = self.items[self.i % len(self.items)]
        self.i += 1
        return it


W_OFF = dict(qa=0, ka=1024, va=1280, qb=1536, fb=2560, ib=3584, og=4608, ga=5632, gb=7680)
CW = 1408
TWO_PI = 6.283185307179586


def build(mode="full", ncores=NCORES):
    nc = bass.Bass("TRN2", target_bir_lowering=False)
    S = Sched(nc)
    uid = [0]

    def din(name, shape, dt=F32):
        return nc.dram_tensor(name, list(shape), dt, kind="ExternalInput").ap()

    def dscr(name, shape, dt=F32):
        kind = "ExternalOutput" if (mode.startswith("dbg") and name not in ("st_loc", "st_all")) else "Internal"
        return nc.dram_tensor(name, list(shape), dt, kind=kind).ap()

    x_in = din("x", [TOK + LB, D])
    p_in = din("p", [TOK, PLE])
    pos_in = din("pos", [1, TOK + LB], I32)
    cst_in = din("consts", [128, CW])
    nrm_in = din("norms", [128, 6, 16])
    sml_in = din("small", [128, 64])
    W = {k: din(k, shp) for k, shp in [
        ("ffn1_w_gate", [D, FF]), ("ffn1_w_up", [D, FF]), ("ffn1_w_down", [FF, D]), ("w_in", [D, IN_DIM]),
        ("w_up_a", [1024, D]), ("w_up_b", [1024, D]), ("w_out", [D, D]),
        ("ffn2_w_gate", [D, FF]), ("ffn2_w_up", [D, FF]), ("ffn2_w_down", [FF, D]),
        ("ple_w_gate", [D, D]), ("ple_w_proj", [PLE, D])]}
    w_in = W["w_in"]
    y_out = nc.dram_tensor("y", [TOK, D], F32, kind="ExternalOutput").ap()
    x1s = dscr("x1s", [128, 16, TOK]); x1s_b = [Buf("x1s%d" % i) for i in range(NBLK)]
    oas = dscr("oas", [128, 8, TOK], BF16); oas_b = [Buf("oas%d" % i) for i in range(NBLK)]
    obs = dscr("obs", [128, 8, TOK]); obs_b = [Buf("obs%d" % i) for i in range(NBLK)]
    qts = dscr("qts", [128, 8, TOK], BF16); qts_b = [Buf("qts%d" % i) for i in range(NBLK)]
    st_loc = dscr("st_loc", [128, 1024]); st_loc_b = Buf("st_loc")
    st_all = dscr("st_all", [2 * 128, 1024]); st_all_b = Buf("st_all")
    ybuf = Buf("ydram")
    dbg_bufs = []

    def mk(alloc, name, shape, dt):
        uid[0] += 1
        nm = "%s_%d" % (name, uid[0])
        return T(alloc(nm, list(shape), dt), nm)

    def sb(name, shape, dt):
        return mk(nc.alloc_sbuf_tensor, name, shape, dt)

    def ps(name, shape, dt=F32):
        return mk(nc.alloc_psum_tensor, name, shape, dt)

    class Scope:
        def __init__(self):
            self.st = ExitStack()
            self.items = []

        def sb(self, name, shape, dt):
            uid[0] += 1
            nm = "%s_%d" % (name, uid[0])
            t = T(self.st.enter_context(nc.sbuf_tensor(nm, list(shape), dt)), nm)
            self.items.append(t)
            return t

        def close(self):
            barrier(self.items)
            for t in self.items:
                if t.buf.dsem is not None:
                    S.sem_pool.append(t.buf.dsem)
                    t.buf.dsem = None
            self.st.close()

    cst = sb("cst", [128, CW], F32)
    nrm = sb("nrm", [128, 6, 16], F32)
    sml = sb("sml", [128, 64], F32)
    ident_f = cst.t[:, 0:128]
    maskB = cst.t[:, 128:384]
    maskA = cst.t[:, 384:640]
    prot = cst.t[:, 640:768]
    resetm = cst.t[:, 768:1280]
    bdmask = cst.t[:, 1280:1408]
    ident_b = sb("ident_b", [128, 128], BF16)
    ones_b = sb("ones_b", [128, 128], BF16)
    cc = sb("cc", [128, 64], F32)
    epsc = cc.t[:, 0:1]; pic = cc.t[:, 1:2]
    lbc = cc.t[:, 8:16]; omlb = cc.t[:, 16:24]; nomlb = cc.t[:, 24:32]; negsink = cc.t[:, 32:48]
    hgw = sml.t[:, 16:24]; sink = sml.t[:, 24:40]; sel = sml.t[:, 40:48]; invf = sml.t[:, 48:49]
    zeros_f = sb("zeros_f", [128, 512], F32)
    bar = sb("bar", [128, 1], F32)
    wslots = Ring([sb("wslot%d" % i, [128, WSLOT], BF16) for i in range(NWSLOT)])
    hT = sb("hT", [128, 16, TB + LB], BF16)
    sq = Ring([sb("sq%d" % i, [128, TB + LB], BF16) for i in range(2)])
    sg = Ring([sb("sg%d" % i, [128, TB + LB], F32) for i in range(2)])
    rstd = sb("rstd", [128, TB + LB], F32)
    kT = sb("kT", [128, 4, LB + TB], BF16)
    Vpad = sb("Vpad", [128, 5, 4, 2, 128], BF16)
    S_f = sb("S_f", [128, 8, 128], F32)
    S_bf = sb("S_bf", [128, 8, 128], BF16)
    carry = sb("carry", [128, 8], F32)
    accs = Ring([ps("acc%d" % i, [128, 1024]) for i in range(2)])
    aux = Ring([ps("aux%d" % i, [128, 512]) for i in range(3)])
    auxb = ps("auxb", [128, 1024], BF16)

    def barrier(items):
        for e in ("pe", "act", "pool"):
            if S.cnt[e] > 0:
                S._wait("dve", e, ("e", e, S.cnt[e]))
        for t in items:
            S.deps("dve", [], [t.buf])
        S.op("dve", [], [bar.buf], lambda e: e.memset(bar.t[:], 0.0))
        m = S.cnt["dve"]
        for e in ("act", "sp"):
            S._wait(e, "dve", ("e", "dve", m))

    def dump(name, src):
        shp = list(src.t.shape)
        o = nc.dram_tensor("dbg_" + name, shp, F32, kind="ExternalOutput").ap()
        b = Buf("dbg_" + name)
        idx = tuple(slice(None) for _ in shp)
        S.dma("pool", o[idx], src.t[idx], [src.buf], [b], b)
        dbg_bufs.append(b)

    S.dma("sp", cst.t[:], cst_in[:, :], [], [cst.buf], cst.buf)
    S.dma("sp", nrm.t[:], nrm_in[:, :, :], [], [nrm.buf], nrm.buf)
    S.dma("sp", sml.t[:], sml_in[:, :], [], [sml.buf], sml.buf)
    S.op("dve", [cst.buf], [ident_b.buf], lambda e: e.tensor_copy(ident_b.t[:], ident_f))
    S.op("dve", [], [ones_b.buf], lambda e: e.memset(ones_b.t[:], 1.0))
    S.op("dve", [], [cc.buf], lambda e: e.memset(cc.t[:, 0:1], EPS))
    S.op("dve", [], [cc.buf], lambda e: e.memset(cc.t[:, 1:2], float(np.pi)))
    S.op("dve", [], [zeros_f.buf], lambda e: e.memset(zeros_f.t[:], 0.0))
    S.op("dve", [], [Vpad.buf], lambda e: e.memset(Vpad.t[:], 0.0))
    S.op("dve", [], [S_f.buf], lambda e: e.memset(S_f.t[:], 0.0))
    S.op("dve", [], [S_bf.buf], lambda e: e.memset(S_bf.t[:], 0.0))
    S.op("dve", [], [carry.buf], lambda e: e.memset(carry.t[:], 0.0))
    S.op("dve", [sml.buf], [cc.buf], lambda e: e.tensor_tensor(out=lbc, in0=sml.t[:, 0:8], in1=sml.t[:, 8:16], op=ALU.subtract))
    S.op("act", [cc.buf], [cc.buf], lambda e: e.activation(out=lbc, in_=lbc, func=AF.Sigmoid))
    S.op("dve", [cc.buf], [cc.buf], lambda e: e.tensor_scalar(out=omlb, in0=lbc, scalar1=-1.0, scalar2=1.0, op0=ALU.mult, op1=ALU.add))
    S.op("dve", [cc.buf], [cc.buf], lambda e: e.tensor_scalar(out=nomlb, in0=lbc, scalar1=-1.0, scalar2=None, op0=ALU.add))
    S.op("dve", [sml.buf], [cc.buf], lambda e: e.tensor_scalar(out=negsink, in0=sink, scalar1=-1.0, scalar2=None, op0=ALU.mult))

    def splits_of(n):
        return [(0, 512)] + ([(512, n)] if n > 512 else [])

    def transpose_in(src_dram_rows, ncol_chunks, xt, dst, c0, dst_is_bf16=False):
        S.dma("sp", xt.t[:, 0:ncol_chunks * 128], src_dram_rows, [], [xt.buf], xt.buf)
        for g in range((ncol_chunks + 3) // 4):
            a = aux.next()
            k = min(4, ncol_chunks - g * 4)
            for i in range(k):
                dc = g * 4 + i
                S.tr(a.t[:, i * 128:(i + 1) * 128], xt.t[:, dc * 128:(dc + 1) * 128], ident_f,
                     [xt.buf, cst.buf], [a.buf], sig=(i == k - 1))
            src = a.t[:, 0:k * 128].rearrange("p (i t) -> p i t", i=k)
            d = dst.t[:, g * 4:g * 4 + k, c0:c0 + 128]
            if g % 2 == 0:
                S.op("act", [a.buf], [dst.buf], lambda e: e.activation(out=d, in_=src, func=AF.Copy))
            else:
                S.op("dve", [a.buf], [dst.buf], lambda e: e.tensor_copy(d, src))

    def rmsnorm_T(src, n, widx, dst):
        spl = splits_of(n)
        a = accs.next()
        for dc in range(16):
            s = sq.next()
            S.op("act", [src.buf], [s.buf], lambda e: e.activation(out=s.t[:, :n], in_=src.t[:, dc, :n], func=AF.Square))
            for (c0, c1) in spl:
                S.mm(a.t[:, c0:c1], ones_b.t[:], s.t[:, c0:c1], dc == 0, dc == 15, [ones_b.buf, s.buf], [a.buf], sig=True)
        S.op("act", [a.buf, cc.buf], [rstd.buf],
             lambda e: e.activation(out=rstd.t[:, :n], in_=a.t[:, :n], func=AF.Sqrt, scale=1.0 / D, bias=epsc))
        S.op("dve", [rstd.buf], [rstd.buf], lambda e: e.reciprocal(rstd.t[:, :n], rstd.t[:, :n]))
        for dc in range(16):
            S.op("dve", [src.buf, rstd.buf, nrm.buf], [dst.buf],
                 lambda e: e.scalar_tensor_tensor(out=dst.t[:, dc, :n], in0=src.t[:, dc, :n],
                                                  scalar=nrm.t[:, widx, dc:dc + 1], in1=rstd.t[:, :n],
                                                  op0=ALU.mult, op1=ALU.mult))

    def plain(w, col0, K, fgroup):
        KC = K // 128

        def load(g, sl):
            c = col0 + g * fgroup
            S.dma("pool", sl.t[:, 0:KC * fgroup].rearrange("p (kc f) -> p kc f", f=fgroup),
                  w[:, c:c + fgroup].rearrange("(kc p) f -> p kc f", p=128), [], [sl.buf], sl.buf)
        return load

    def gemm(srcs, ngroups, fgroup, n, epi):
        spl = splits_of(n)
        for g in range(ngroups):
            slots = []
            for (load, K, rhs, rbufs) in srcs:
                assert (K // 128) * fgroup <= WSLOT
                sl = wslots.next()
                load(g, sl)
                slots.append(sl)
            for fc in range(fgroup // 128):
                for wi, sl in enumerate(slots):
                    (load, K, rhs, rbufs) = srcs[wi]
                    KC = K // 128
                    a = accs.next()
                    for (c0, c1) in spl:
                        for kc in range(KC):
                            o = kc * fgroup + fc * 128
                            S.mm(a.t[:, c0:c1], sl.t[:, o:o + 128], rhs(kc, c0, c1), kc == 0, kc == KC - 1,
                                 [sl.buf] + rbufs, [a.buf], sig=(kc == KC - 1))
                    epi(wi, g * (fgroup // 128) + fc, a)

    def gemm_tm(w, col0, K, cols, cgroup, lhs, lbufs, tiles, epi):
        KC = K // 128
        for g in range(cols // cgroup):
            sl = wslots.next()
            plain(w, col0, K, cgroup)(g, sl)
            for tl in tiles:
                a = accs.next()
                for kc in range(KC):
                    S.mm(a.t[:, 0:cgroup], lhs(kc, tl), sl.t[:, kc * cgroup:(kc + 1) * cgroup], kc == 0, kc == KC - 1,
                         [sl.buf] + lbufs, [a.buf], sig=(kc == KC - 1))
                epi(tl, g, a)

    def ffn(xT, actT, n, widx, wg, wu, wd):
        rmsnorm_T(xT, n, widx, hT)
        for (fa, fb) in FSPLIT:
            cur = {}

            def epi_gu(wi, fc, a):
                if wi == 0:
                    s = sg.next()
                    cur["s"] = s
                    S.op("act", [a.buf], [s.buf], lambda e: e.activation(out=s.t[:, :n], in_=a.t[:, :n], func=AF.Silu))
                else:
                    s = cur["s"]
                    S.op("dve", [a.buf, s.buf], [actT.buf],
                         lambda e: e.tensor_tensor(out=actT.t[:, fc, :n], in0=a.t[:, :n], in1=s.t[:, :n], op=ALU.mult))

            rh = (lambda kc, c0, c1: hT.t[:, kc, c0:c1])
            gemm([(plain(wg, fa, D, 512), D, rh, [hT.buf]), (plain(wu, fa, D, 512), D, rh, [hT.buf])],
                 (fb - fa) // 512, 512, n, epi_gu)

            def epi_d(wi, fc, a):
                S.op("dve", [a.buf, xT.buf], [xT.buf],
                     lambda e: e.scalar_tensor_tensor(out=xT.t[:, fc, :n], in0=a.t[:, :n], scalar=0.5,
                                                      in1=xT.t[:, fc, :n], op0=ALU.mult, op1=ALU.add))

            gemm([(plain(wd[fa:fb, :], 0, fb - fa, 256), fb - fa, lambda kc, c0, c1: actT.t[:, kc, c0:c1], [actT.buf])],
                 D // 256, 256, n, epi_d)

    def store_out(blk, src, xt):
        for t in range(TB // 128):
            for g in range(4):
                a = aux.next()
                for i in range(4):
                    dc = g * 4 + i
                    S.tr(a.t[:, i * 128:(i + 1) * 128], src.t[:, dc, t * 128:(t + 1) * 128], ident_f,
                         [src.buf, cst.buf], [a.buf], sig=(i == 3))
                dst = xt.t[:, g * 512:(g + 1) * 512]
                if g % 2 == 0:
                    S.op("act", [a.buf], [xt.buf], lambda e: e.activation(out=dst, in_=a.t[:, :], func=AF.Copy))
                else:
                    S.op("dve", [a.buf], [xt.buf], lambda e: e.tensor_copy(dst, a.t[:, :]))
            r0 = blk * TB + t * 128
            S.dma("sp", y_out[r0:r0 + 128, :], xt.t[:], [xt.buf], [ybuf], xt.buf)

    def phase_a(blk):
        n = TB + LB if blk == 0 else TB
        sc = Scope()
        xT = sc.sb("xT", [128, 16, TB + LB], F32)
        xt = sc.sb("xtok", [128, D], F32)
        actT = sc.sb("actT", [128, FH // 128, TB + LB], BF16)
        tiles = [(blk * TB + t * 128, t * 128) for t in range(4)]
        if blk == 0:
            tiles.append((TOK, TB))
        for (r0, c0) in tiles:
            transpose_in(x_in[r0:r0 + 128, :], 16, xt, xT, c0)
        ffn(xT, actT, n, 0, W["ffn1_w_gate"], W["ffn1_w_up"], W["ffn1_w_down"])
        rmsnorm_T(xT, n, 1, hT)
        S.dma("sp", x1s[:, :, blk * TB:(blk + 1) * TB], xT.t[:, :, 0:TB], [xT.buf], [x1s_b[blk]], xT.buf)
        if mode == "dbgA" and blk == 0:
            dump("x1T", xT)
            dump("hT", hT)
        sc.close()
        if mode == "ffn_only":
            return
        mixer_a(blk, n)

    def mixer_a(blk, n):
        sc = Scope()
        posi = sc.sb("posi", [128, TB + LB], I32)
        cosT = sc.sb("cosT", [128, TB + LB], F32)
        sinT = sc.sb("sinT", [128, TB + LB], F32)
        S.dma("sp", posi.t[:, 0:TB], pos_in[0:1, blk * TB:(blk + 1) * TB].broadcast_to([128, TB]), [], [posi.buf], posi.buf)
        if blk == 0:
            S.dma("sp", posi.t[:, TB:n], pos_in[0:1, TOK:TOK + LB].broadcast_to([128, LB]), [], [posi.buf], posi.buf)
        S.op("dve", [posi.buf], [sinT.buf], lambda e: e.tensor_copy(sinT.t[:, :n], posi.t[:, :n]))
        S.op("dve", [sinT.buf, sml.buf], [sinT.buf], lambda e: e.tensor_scalar(out=sinT.t[:, :n], in0=sinT.t[:, :n], scalar1=invf, scalar2=None, op0=ALU.mult))
        S.op("dve", [sinT.buf], [cosT.buf], lambda e: e.tensor_scalar(out=cosT.t[:, :n], in0=sinT.t[:, :n], scalar1=float(1.0 / TWO_PI), scalar2=12582912.0, op0=ALU.mult, op1=ALU.add))
        S.op("dve", [cosT.buf], [cosT.buf], lambda e: e.tensor_scalar(out=cosT.t[:, :n], in0=cosT.t[:, :n], scalar1=-12582912.0, scalar2=None, op0=ALU.add))
        S.op("dve", [cosT.buf, sinT.buf], [sinT.buf], lambda e: e.scalar_tensor_tensor(out=sinT.t[:, :n], in0=cosT.t[:, :n], scalar=-6.28125, in1=sinT.t[:, :n], op0=ALU.mult, op1=ALU.add))
        S.op("dve", [cosT.buf, sinT.buf], [sinT.buf], lambda e: e.scalar_tensor_tensor(out=sinT.t[:, :n], in0=cosT.t[:, :n], scalar=-(TWO_PI - 6.28125), in1=sinT.t[:, :n], op0=ALU.mult, op1=ALU.add))
        S.op("dve", [sinT.buf], [sinT.buf], lambda e: e.tensor_scalar(out=sinT.t[:, :n], in0=sinT.t[:, :n], scalar1=-3.1415925, scalar2=3.1415925, op0=ALU.max, op1=ALU.min))
        S.op("act", [sinT.buf], [cosT.buf], lambda e: e.activation(out=cosT.t[:, :n], in_=sinT.t[:, :n], func=AF.Sin, scale=0.5))
        S.op("act", [sinT.buf], [sinT.buf], lambda e: e.activation(out=sinT.t[:, :n], in_=sinT.t[:, :n], func=AF.Sin))
        S.op("dve", [cosT.buf], [cosT.buf], lambda e: e.tensor_tensor(out=cosT.t[:, :n], in0=cosT.t[:, :n], in1=cosT.t[:, :n], op=ALU.mult))
        S.op("dve", [cosT.buf], [cosT.buf], lambda e: e.tensor_scalar(out=cosT.t[:, :n], in0=cosT.t[:, :n], scalar1=-2.0, scalar2=1.0, op0=ALU.mult, op1=ALU.add))
        qf = Ring([sc.sb("qf%d" % i, [128, TB + LB], F32) for i in range(2)])
        rt = Ring([sc.sb("rt%d" % i, [128, TB + LB], F32) for i in range(2)])

        def rope(a, nn, writes):
            f = qf.next(); r = rt.next()
            S.op("act", [a.buf], [f.buf], lambda e: e.activation(out=f.t[:, :nn], in_=a.t[:, :nn], func=AF.Copy))
            a2 = accs.next()
            for (c0, c1) in splits_of(nn):
                S.mm(a2.t[:, c0:c1], prot, f.t[:, c0:c1], True, True, [cst.buf, f.buf], [a2.buf], sig=True)
            S.op("dve", [a2.buf, sinT.buf], [r.buf], lambda e: e.tensor_tensor(out=r.t[:, :nn], in0=a2.t[:, :nn], in1=sinT.t[:, :nn], op=ALU.mult))
            S.op("dve", [f.buf, cosT.buf], [f.buf], lambda e: e.tensor_tensor(out=f.t[:, :nn], in0=f.t[:, :nn], in1=cosT.t[:, :nn], op=ALU.mult))
            return f, r

        rh = (lambda kc, c0, c1: hT.t[:, kc, c0:c1])
        def load_k(g, sl):
            for half in range(2):
                c = W_OFF["ka"] + g * 64
                S.dma("pool", sl.t[:, 0:16 * 128].rearrange("p (kc f) -> p kc f", f=128)[:, :, half * 64:(half + 1) * 64],
                      w_in[:, c:c + 64].rearrange("(kc p) f -> p kc f", p=128), [], [sl.buf], sl.buf)

        def epi_k(wi, g, a):
            f, r = rope(a, n, None)
            S.op("dve", [f.buf, r.buf], [kT.buf], lambda e: e.tensor_tensor(out=kT.t[:, g, LB:LB + TB], in0=f.t[:, 0:TB], in1=r.t[:, 0:TB], op=ALU.add))
            if n > TB:
                S.op("dve", [f.buf, r.buf], [kT.buf], lambda e: e.tensor_tensor(out=kT.t[:, g, 0:LB], in0=f.t[:, TB:n], in1=r.t[:, TB:n], op=ALU.add))

        gemm([(load_k, D, rh, [hT.buf])], 4, 128, n, epi_k)

        vtiles = [0, 1, 2, 3] + ([4] if blk == 0 else [])

        def epi_v(tl, g, a):
            vt = 0 if tl == 4 else tl + 1
            src = a.t[:, 0:256].rearrange("p (g d) -> p g d", g=4)
            S.op("act", [a.buf], [Vpad.buf], lambda e: e.activation(out=Vpad.t[:, vt, :, 0, 0:64], in_=src, func=AF.Copy))
            S.op("dve", [a.buf], [Vpad.buf], lambda e: e.tensor_copy(Vpad.t[:, vt, :, 1, 64:128], src))

        gemm_tm(w_in, W_OFF["va"], D, 256, 256, lambda kc, tl: hT.t[:, kc, tl * 128:(tl + 1) * 128], [hT.buf], vtiles, epi_v)

        qrot = [sc.sb("qrot%d" % i, [128, TB], BF16) for i in range(8)]
        oaT = [sc.sb("oaT%d" % i, [128, TB], BF16) for i in range(8)]
        ssb = Ring([sc.sb("ssb%d" % i, [128, 256], F32) for i in range(3)])
        eb = Ring([sc.sb("eb%d" % i, [128, 256], F32) for i in range(3)])
        pb = Ring([sc.sb("pb%d" % i, [128, 256], BF16) for i in range(4)])
        pt = Ring([sc.sb("pt%d" % i, [128, 256], BF16) for i in range(4)])
        sm = Ring([sc.sb("sm%d" % i, [128, 8], F32) for i in range(6)])

        def epi_q(wi, fc, a):
            f, r = rope(a, TB, None)
            q = qrot[fc]
            S.op("dve", [f.buf, r.buf], [q.buf], lambda e: e.tensor_tensor(out=q.t[:, :], in0=f.t[:, 0:TB], in1=r.t[:, 0:TB], op=ALU.add))

        gemm([(plain(w_in, W_OFF["qa"], D, 512), D, rh, [hT.buf])], 2, 512, TB, epi_q)

        class View:
            def __init__(self, t, buf):
                self.t = t
                self.buf = buf

        sc_slots = Ring([View(aux.items[b].t[:, 0:256], aux.items[b].buf) for b in range(2)])
        oa_slots = Ring([View(aux.items[2].t[:, 0:128], aux.items[2].buf)])
        tb_slots = Ring([View(auxb.t[:, 0:256], auxb.buf)])
        units = [(g, j, pair, hh) for g in range(4) for j in range(4) for pair in range(2) for hh in range(2)]
        st = {}

        def stA(u):
            g, j, pair, hh = units[u]
            h = 4 * g + 2 * pair + hh
            base = hh * 64
            q = qrot[2 * g + pair]
            msk = maskA if (blk == 0 and j == 0) else maskB
            a = sc_slots.next()
            S.mm(a.t, q.t[base:base + 64, j * 128:(j + 1) * 128], kT.t[base:base + 64, g, j * 128:j * 128 + 256],
                 True, True, [q.buf, kT.buf], [a.buf], sig=True)
            s_ = ssb.next(); e_ = eb.next(); p_ = pb.next(); m_ = sm.next()
            S.op("dve", [a.buf, cst.buf], [s_.buf], lambda e: e.tensor_tensor(out=s_.t[:, :], in0=a.t, in1=msk, op=ALU.add))
            S.op("dve", [s_.buf], [m_.buf], lambda e: e.reduce_max(out=m_.t[:, 0:1], in_=s_.t[:, :], axis=AX.X))
            S.op("dve", [m_.buf, cc.buf], [m_.buf], lambda e: e.tensor_scalar(out=m_.t[:, 1:2], in0=m_.t[:, 0:1], scalar1=-0.125, scalar2=negsink[:, h:h + 1], op0=ALU.mult, op1=ALU.min))
            S.op("act", [s_.buf, m_.buf], [e_.buf], lambda e: e.activation(out=e_.t[:, :], in_=s_.t[:, :], func=AF.Exp, scale=0.125, bias=m_.t[:, 1:2]))
            S.op("act", [m_.buf, sml.buf], [m_.buf], lambda e: e.activation(out=m_.t[:, 3:4], in_=m_.t[:, 1:2], func=AF.Exp, scale=1.0, bias=sink[:, h:h + 1]))
            S.op("dve", [e_.buf], [m_.buf], lambda e: e.reduce_sum(out=m_.t[:, 2:3], in_=e_.t[:, :], axis=AX.X))
            S.op("dve", [m_.buf], [m_.buf], lambda e: e.tensor_tensor(out=m_.t[:, 4:5], in0=m_.t[:, 2:3], in1=m_.t[:, 3:4], op=ALU.add))
            S.op("dve", [m_.buf], [m_.buf], lambda e: e.reciprocal(m_.t[:, 5:6], m_.t[:, 4:5]))
            S.op("dve", [e_.buf, m_.buf], [p_.buf], lambda e: e.tensor_scalar(out=p_.t[:, :], in0=e_.t[:, :], scalar1=m_.t[:, 5:6], scalar2=None, op0=ALU.mult))
            st[u] = [p_]

        def stB(u):
            p_ = st[u][0]
            tb = tb_slots.next(); t_ = pt.next()
            S.tr(tb.t[:, 0:128], p_.t[:, 0:128], ident_b.t[:], [p_.buf, ident_b.buf], [tb.buf], sig=False)
            S.tr(tb.t[:, 128:256], p_.t[:, 128:256], ident_b.t[:], [p_.buf, ident_b.buf], [tb.buf], sig=True)
            S.op("act", [tb.buf], [t_.buf], lambda e: e.activation(out=t_.t[:, :], in_=tb.t, func=AF.Copy))
            st[u] = [t_]

        def stC(u):
            g, j, pair, hh = units[u]
            t_ = st.pop(u)[0]
            if hh == 0:
                st["oacc"] = oa_slots.next()
            oacc = st["oacc"]
            for kb in range(2):
                S.mm(oacc.t, Vpad.t[:, j + kb, g, hh, :], t_.t[:, kb * 128:(kb + 1) * 128],
                     (hh == 0 and kb == 0), (hh == 1 and kb == 1), [Vpad.buf, t_.buf], [oacc.buf], sig=(kb == 1))
            if hh == 1:
                o = oaT[2 * g + pair]
                S.op("act", [oacc.buf], [o.buf], lambda e: e.activation(out=o.t[:, j * 128:(j + 1) * 128], in_=oacc.t, func=AF.Copy))
                if j == 3:
                    S.dma("sp", oas[:, 2 * g + pair, blk * TB:(blk + 1) * TB], o.t[:, :], [o.buf], [oas_b[blk]], o.buf)

        NU = len(units); SK1 = 2; SK2 = 1
        for i in range(NU + SK1 + SK2):
            if i < NU:
                stA(i)
            if 0 <= i - SK1 < NU:
                stB(i - SK1)
            if 0 <= i - SK1 - SK2 < NU:
                stC(i - SK1 - SK2)
        S.op("dve", [kT.buf], [kT.buf], lambda e: e.tensor_copy(kT.t[:, :, 0:LB], kT.t[:, :, TB:TB + LB]))
        S.op("dve", [Vpad.buf], [Vpad.buf], lambda e: e.tensor_copy(Vpad.t[:, 0], Vpad.t[:, 4]))
        if mode == "dbgA" and blk == 0:
            dump("kT", kT)
        sc.close()
        if mode == "attn_only":
            return
        sc = Scope()

        vb = sc.sb("vb", [128, 4, 1024], BF16)

        def epi_ib(tl, g, a):
            d = vb.t[:, tl, g * 512:(g + 1) * 512]
            if tl % 2 == 0:
                S.op("act", [a.buf], [vb.buf], lambda e: e.activation(out=d, in_=a.t[:, 0:512], func=AF.Copy))
            else:
                S.op("dve", [a.buf], [vb.buf], lambda e: e.tensor_copy(d, a.t[:, 0:512]))

        gemm_tm(w_in, W_OFF["ib"], D, 1024, 512, lambda kc, tl: hT.t[:, kc, tl * 128:(tl + 1) * 128], [hT.buf], [0, 1, 2, 3], epi_ib)

        qtT = sc.sb("qtT", [128, 8, TB], BF16)
        ktT = sc.sb("ktT", [128, 8, TB], BF16)
        elast = sc.sb("elast", [128, 8, 8], F32)
        qs = Ring([sc.sb("qs%d" % i, [128, TB], F32) for i in range(2)])
        tsg = Ring([sc.sb("tsg%d" % i, [128, TB], F32) for i in range(2)])
        tfg = Ring([sc.sb("tfg%d" % i, [128, TB], F32) for i in range(2)])
        tcu = Ring([sc.sb("tcu%d" % i, [128, TB], F32) for i in range(2)])
        tct = Ring([sc.sb("tct%d" % i, [128, TB], F32) for i in range(1)])
        tex = Ring([sc.sb("tex%d" % i, [128, TB], F32) for i in range(2)])
        qto = Ring([sc.sb("qto%d" % i, [128, TB], BF16) for i in range(2)])
        cur = {}

        def epi_h(wi, hd, a):
            if wi == 0:
                q_ = qs.next(); cur["q"] = q_
                S.op("act", [a.buf], [q_.buf], lambda e: e.activation(out=q_.t[:, :], in_=a.t[:, 0:TB], func=AF.Silu))
                return
            q_ = cur["q"]
            sg_ = tsg.next(); fg_ = tfg.next(); cu_ = tcu.next(); ct_ = tct.next()
            S.op("act", [a.buf], [sg_.buf], lambda e: e.activation(out=sg_.t[:, :], in_=a.t[:, 0:TB], func=AF.Sigmoid))
            S.op("dve", [sg_.buf, cc.buf], [fg_.buf], lambda e: e.tensor_scalar(out=fg_.t[:, :], in0=sg_.t[:, :], scalar1=omlb[:, hd:hd + 1], scalar2=lbc[:, hd:hd + 1], op0=ALU.mult, op1=ALU.add))
            S.op("act", [fg_.buf], [fg_.buf], lambda e: e.activation(out=fg_.t[:, :], in_=fg_.t[:, :], func=AF.Ln))
            S.op("dve", [sg_.buf, cc.buf], [sg_.buf], lambda e: e.tensor_scalar(out=sg_.t[:, :], in0=sg_.t[:, :], scalar1=nomlb[:, hd:hd + 1], scalar2=omlb[:, hd:hd + 1], op0=ALU.mult, op1=ALU.add))
            S.op("dve", [fg_.buf, cst.buf], [cu_.buf], lambda e: e.tensor_tensor_scan(out=cu_.t[:, :], data0=resetm, data1=fg_.t[:, :], initial=0.0, op0=ALU.mult, op1=ALU.add))
            S.op("dve", [fg_.buf, zeros_f.buf, carry.buf], [ct_.buf], lambda e: e.tensor_tensor_scan(out=ct_.t[:, :], data0=zeros_f.t[:, :], data1=fg_.t[:, :], initial=carry.t[:, hd:hd + 1], op0=ALU.add, op1=ALU.add))
            S.op("dve", [ct_.buf], [carry.buf], lambda e: e.tensor_copy(carry.t[:, hd:hd + 1], ct_.t[:, TB - 1:TB]))
            x1 = tex.next()
            S.op("act", [cu_.buf], [x1.buf], lambda e: e.activation(out=x1.t[:, :], in_=cu_.t[:, :], func=AF.Exp))
            S.op("dve", [q_.buf, x1.buf], [qtT.buf], lambda e: e.tensor_tensor(out=qtT.t[:, hd, :], in0=q_.t[:, :], in1=x1.t[:, :], op=ALU.mult))
            x2 = tex.next()
            S.op("act", [cu_.buf], [x2.buf], lambda e: e.activation(out=x2.t[:, :], in_=cu_.t[:, :], func=AF.Exp, scale=-1.0))
            S.op("dve", [sg_.buf, x2.buf], [ktT.buf], lambda e: e.tensor_tensor(out=ktT.t[:, hd, :], in0=sg_.t[:, :], in1=x2.t[:, :], op=ALU.mult))
            x3 = tex.next(); qo = qto.next()
            S.op("act", [ct_.buf], [x3.buf], lambda e: e.activation(out=x3.t[:, :], in_=ct_.t[:, :], func=AF.Exp))
            S.op("dve", [q_.buf, x3.buf], [qo.buf], lambda e: e.tensor_tensor(out=qo.t[:, :], in0=q_.t[:, :], in1=x3.t[:, :], op=ALU.mult))
            S.dma("sp", qts[:, hd, blk * TB:(blk + 1) * TB], qo.t[:, :], [qo.buf], [qts_b[blk]], qo.buf)
            S.op("act", [cu_.buf], [elast.buf], lambda e: e.activation(out=elast.t[:, hd, :], in_=cu_.t[:, :].rearrange("p (c t) -> p c t", t=64)[:, :, 63], func=AF.Exp))

        gemm([(plain(w_in, W_OFF["qb"], D, 512), D, rh, [hT.buf]), (plain(w_in, W_OFF["fb"], D, 512), D, rh, [hT.buf])], 2, 512, TB, epi_h)
        if mode == "dbgA" and blk == 0:
            dump("qtT", qtT); dump("ktT", ktT); dump("elast", elast)

        ktok = Ring([sc.sb("ktok%d" % i, [128, 128], BF16) for i in range(8)])
        AT = Ring([sc.sb("AT%d" % i, [128, 128], BF16) for i in range(8)])
        stmp = sc.sb("stmp", [128, 8, 128], F32)
        ol = Ring([sc.sb("ol%d" % i, [128, 8, 128], F32) for i in range(2)])
        for pr in range(4):
            oac = accs.next()
            su = accs.next()
            kts = []
            c0 = pr * 128
            ats = []
            for hd in range(8):
                S.tr(auxb.t[:, hd * 128:(hd + 1) * 128], ktT.t[:, hd, c0:c0 + 128], ident_b.t[:], [ktT.buf, ident_b.buf], [auxb.buf], sig=(hd % 4 == 3))
            for hd in range(8):
                kk = ktok.next(); kts.append(kk)
                S.op("act", [auxb.buf], [kk.buf], lambda e: e.activation(out=kk.t[:, :], in_=auxb.t[:, hd * 128:(hd + 1) * 128], func=AF.Copy))
            for hd in range(8):
                a = aux.items[hd // 4]
                S.mm(a.t[:, (hd % 4) * 128:(hd % 4 + 1) * 128], ktT.t[:, hd, c0:c0 + 128], qtT.t[:, hd, c0:c0 + 128], True, True, [ktT.buf, qtT.buf], [a.buf], sig=(hd % 4 == 3))
            for hd in range(8):
                a = aux.items[hd // 4]
                at = AT.next(); ats.append(at)
                S.op("dve", [a.buf, cst.buf], [at.buf], lambda e: e.tensor_tensor(out=at.t[:, :], in0=a.t[:, (hd % 4) * 128:(hd % 4 + 1) * 128], in1=bdmask, op=ALU.mult))
            for hd in range(8):
                kk = kts[hd]; at = ats[hd]
                o_ = oac.t[:, hd * 128:(hd + 1) * 128]
                S.mm(o_, vb.t[:, pr, hd * 128:(hd + 1) * 128], at.t[:, :], hd % 4 == 0, False, [vb.buf, at.buf], [oac.buf], sig=False, skip=True)
                S.mm(oac.t[:, hd * 128:hd * 128 + 64], S_bf.t[:, hd, :], qtT.t[:, hd, c0:c0 + 64], False, False, [S_bf.buf, qtT.buf], [oac.buf], sig=False, skip=True)
                S.mm(su.t[:, hd * 128:(hd + 1) * 128], kk.t[0:64, :], vb.t[0:64, pr, hd * 128:(hd + 1) * 128], True, True, [kk.buf, vb.buf], [su.buf], sig=(hd == 7))
            S.op("dve", [su.buf, S_f.buf], [stmp.buf], lambda e: e.tensor_tensor(out=stmp.t[:, :, :], in0=su.t[:, :].rearrange("p (h e) -> p h e", h=8), in1=S_f.t[:, :, :], op=ALU.add))
            for hd in range(8):
                S.op("dve", [stmp.buf, elast.buf], [S_f.buf], lambda e: e.tensor_scalar(out=S_f.t[:, hd, :], in0=stmp.t[:, hd, :], scalar1=elast.t[:, hd, 2 * pr:2 * pr + 1], scalar2=None, op0=ALU.mult))
                S.op("act", [stmp.buf, elast.buf], [S_bf.buf], lambda e: e.activation(out=S_bf.t[:, hd, :], in_=stmp.t[:, hd, :], func=AF.Copy, scale=elast.t[:, hd, 2 * pr:2 * pr + 1]))
            su2 = su
            for hd in range(8):
                kk = kts[hd]
                S.mm(oac.t[:, hd * 128 + 64:hd * 128 + 128], S_bf.t[:, hd, :], qtT.t[:, hd, c0 + 64:c0 + 128], False, True, [S_bf.buf, qtT.buf], [oac.buf], sig=(hd == 7), skip=True)
                S.mm(su2.t[:, hd * 128:(hd + 1) * 128], kk.t[64:128, :], vb.t[64:128, pr, hd * 128:(hd + 1) * 128], True, True, [kk.buf, vb.buf], [su2.buf], sig=(hd == 7))
            S.op("dve", [su2.buf, S_f.buf], [stmp.buf], lambda e: e.tensor_tensor(out=stmp.t[:, :, :], in0=su2.t[:, :].rearrange("p (h e) -> p h e", h=8), in1=S_f.t[:, :, :], op=ALU.add))
            for hd in range(8):
                S.op("dve", [stmp.buf, elast.buf], [S_f.buf], lambda e: e.tensor_scalar(out=S_f.t[:, hd, :], in0=stmp.t[:, hd, :], scalar1=elast.t[:, hd, 2 * pr + 1:2 * pr + 2], scalar2=None, op0=ALU.mult))
                S.op("act", [stmp.buf, elast.buf], [S_bf.buf], lambda e: e.activation(out=S_bf.t[:, hd, :], in_=stmp.t[:, hd, :], func=AF.Copy, scale=elast.t[:, hd, 2 * pr + 1:2 * pr + 2]))
            o_sb = ol.next()
            S.op("act", [oac.buf], [o_sb.buf], lambda e: e.activation(out=o_sb.t[:, :, :], in_=oac.t[:, :].rearrange("p (h t) -> p h t", h=8), func=AF.Copy))
            S.dma("sp", obs[:, :, blk * TB + c0:blk * TB + c0 + 128], o_sb.t[:, :, :], [o_sb.buf], [obs_b[blk]], o_sb.buf)
        sc.close()

    def exchange():
        S.dma("sp", st_loc[:, :], S_f.t[:, :, :].rearrange("p h e -> p (h e)"), [S_f.buf], [st_loc_b], S_f.buf)
        S.deps("pool", [st_loc_b], [st_all_b])
        ccsem = nc.alloc_semaphore("ccsem")
        nc.gpsimd.collective_compute("AllGather", ALU.bypass, replica_groups=[[2 * i, 2 * i + 1] for i in range(ncores // 2)],
                                     ins=[st_loc], outs=[st_all]).then_inc(ccsem)
        tok = ("d", ccsem, 1)
        st_all_b.w["cc"] = tok
        st_loc_b.r["cc"] = tok
        sc = Scope()
        sr = Ring([sc.sb("sr%d" % i, [128, 1024], F32) for i in range(2)])
        S.op("dve", [], [S_f.buf], lambda e: e.memset(S_f.t[:], 0.0))
        Sf2 = S_f.t[:, :, :].rearrange("p h e -> p (h e)")
        for r in range(2):
            t = sr.next()
            S.dma("sp", t.t[:, :], st_all[r * 128:(r + 1) * 128, :], [st_all_b], [t.buf], t.buf)
            S.op("dve", [t.buf, S_f.buf, sml.buf], [S_f.buf], lambda e: e.scalar_tensor_tensor(out=Sf2, in0=t.t[:, :], scalar=sel[:, r:r + 1], in1=Sf2, op0=ALU.mult, op1=ALU.add))
        S.op("dve", [S_f.buf], [S_bf.buf], lambda e: e.tensor_copy(S_bf.t[:], S_f.t[:]))
        if mode.startswith("dbg"):
            dump("S_in", S_f)
        sc.close()

    def phase_b(blk):
        n = TB
        cols = slice(blk * TB, (blk + 1) * TB)
        sc0 = Scope()
        xT = sc0.sb("xT", [128, 16, TB], F32)
        S.dma("sp", xT.t[:, :, :], x1s[:, :, cols], [x1s_b[blk]], [xT.buf], xT.buf)
        rmsnorm_T(xT, n, 1, hT)
        rh = (lambda kc, c0, c1: hT.t[:, kc, c0:c1])
        sc1 = Scope()
        outbT = sc1.sb("outbT", [128, 8, TB], BF16)
        sc2 = Scope()
        ob = sc2.sb("ob", [128, 8, TB], F32)
        qtot = sc2.sb("qtot", [128, 8, TB], BF16)
        S.dma("sp", ob.t[:, :, :], obs[:, :, cols], [obs_b[blk]], [ob.buf], ob.buf)
        S.dma("sp", qtot.t[:, :, :], qts[:, :, cols], [qts_b[blk]], [qtot.buf], qtot.buf)
        for hd in range(8):
            a = accs.next()
            S.mm(a.t[:, 0:TB], S_bf.t[:, hd, :], qtot.t[:, hd, :], True, True, [S_bf.buf, qtot.buf], [a.buf], sig=True)
            S.op("dve", [a.buf, ob.buf], [ob.buf], lambda e: e.tensor_tensor(out=ob.t[:, hd, :], in0=a.t[:, 0:TB], in1=ob.t[:, hd, :], op=ALU.add))
            s = sq.next()
            S.op("act", [ob.buf], [s.buf], lambda e: e.activation(out=s.t[:, :TB], in_=ob.t[:, hd, :], func=AF.Square))
            a2 = accs.next()
            S.mm(a2.t[:, 0:TB], ones_b.t[:], s.t[:, 0:TB], True, True, [ones_b.buf, s.buf], [a2.buf], sig=True)
            S.op("act", [a2.buf, cc.buf], [rstd.buf], lambda e: e.activation(out=rstd.t[:, :TB], in_=a2.t[:, :TB], func=AF.Sqrt, scale=1.0 / 128, bias=epsc))
            S.op("dve", [rstd.buf], [rstd.buf], lambda e: e.reciprocal(rstd.t[:, :TB], rstd.t[:, :TB]))
            S.op("dve", [ob.buf, rstd.buf, sml.buf], [ob.buf], lambda e: e.scalar_tensor_tensor(out=ob.t[:, hd, :], in0=ob.t[:, hd, :], scalar=hgw[:, hd:hd + 1], in1=rstd.t[:, :TB], op0=ALU.mult, op1=ALU.mult))

        def epi_og(wi, fc, a):
            s = sg.next()
            S.op("act", [a.buf], [s.buf], lambda e: e.activation(out=s.t[:, :TB], in_=a.t[:, :TB], func=AF.Silu))
            S.op("dve", [ob.buf, s.buf], [outbT.buf], lambda e: e.tensor_tensor(out=outbT.t[:, fc, :], in0=ob.t[:, fc, :], in1=s.t[:, :TB], op=ALU.mult))

        gemm([(plain(w_in, W_OFF["og"], D, 512), D, rh, [hT.buf])], 2, 512, n, epi_og)
        if mode.startswith("dbg") and blk == 0:
            dump("outbT", outbT)
        sc2.close()
        sc3 = Scope()
        oaT = sc3.sb("oaTb", [128, 8, TB], BF16)
        mT = sc3.sb("mT", [128, 16, TB], BF16)
        tA = Ring([sc3.sb("tA%d" % i, [128, TB], F32) for i in range(2)])
        tM = Ring([sc3.sb("tM%d" % i, [128, TB], F32) for i in range(2)])
        S.dma("sp", oaT.t[:, :, :], oas[:, :, cols], [oas_b[blk]], [oaT.buf], oaT.buf)
        cur = {}

        def epi_m(wi, fc, a):
            if wi == 0:
                t = tA.next(); cur["a"] = t
                S.op("act", [a.buf], [t.buf], lambda e: e.activation(out=t.t[:, :], in_=a.t[:, :TB], func=AF.Sigmoid))
            elif wi == 1:
                t = tM.next(); cur["m"] = t
                S.op("dve", [a.buf, cur["a"].buf], [t.buf], lambda e: e.tensor_tensor(out=t.t[:, :], in0=a.t[:, :TB], in1=cur["a"].t[:, :], op=ALU.mult))
            elif wi == 2:
                t = tA.next(); cur["b"] = t
                S.op("act", [a.buf], [t.buf], lambda e: e.activation(out=t.t[:, :], in_=a.t[:, :TB], func=AF.Sigmoid))
            else:
                t = cur["b"]
                S.op("dve", [a.buf, t.buf], [t.buf], lambda e: e.tensor_tensor(out=t.t[:, :], in0=a.t[:, :TB], in1=t.t[:, :], op=ALU.mult))
                S.op("dve", [t.buf, cur["m"].buf], [mT.buf], lambda e: e.tensor_tensor(out=mT.t[:, fc, :], in0=t.t[:, :], in1=cur["m"].t[:, :], op=ALU.add))

        gemm([(plain(w_in, W_OFF["ga"], D, 256), D, rh, [hT.buf]),
              (plain(W["w_up_a"], 0, 1024, 256), 1024, lambda kc, c0, c1: oaT.t[:, kc, c0:c1], [oaT.buf]),
              (plain(w_in, W_OFF["gb"], D, 256), D, rh, [hT.buf]),
              (plain(W["w_up_b"], 0, 1024, 256), 1024, lambda kc, c0, c1: outbT.t[:, kc, c0:c1], [outbT.buf])],
             D // 256, 256, n, epi_m)

        def epi_o(wi, fc, a):
            S.op("dve", [a.buf, xT.buf], [xT.buf], lambda e: e.tensor_tensor(out=xT.t[:, fc, :], in0=a.t[:, :TB], in1=xT.t[:, fc, :], op=ALU.add))

        gemm([(plain(W["w_out"], 0, D, 512), D, lambda kc, c0, c1: mT.t[:, kc, c0:c1], [mT.buf])], D // 512, 512, n, epi_o)
        if mode.startswith("dbg") and blk == 0:
            dump("x2T", xT)
        sc3.close()
        sc1.close()
        sc4 = Scope()
        actT = sc4.sb("actT2", [128, FH // 128, TB], BF16)
        ffn(xT, actT, n, 2, W["ffn2_w_gate"], W["ffn2_w_up"], W["ffn2_w_down"])
        sc4.close()
        sc5 = Scope()
        ptok = sc5.sb("ptok", [128, PLE], F32)
        pT = sc5.sb("pT", [128, 2, TB], BF16)
        for t in range(4):
            r0 = blk * TB + t * 128
            transpose_in(p_in[r0:r0 + 128, :], 2, ptok, pT, t * 128)
        rmsnorm_T(xT, n, 3, hT)
        cur2 = {}

        def epi_p(wi, fc, a):
            if wi == 0:
                s = sg.next(); cur2["s"] = s
                S.op("act", [a.buf], [s.buf], lambda e: e.activation(out=s.t[:, :TB], in_=a.t[:, :TB], func=AF.Sigmoid))
            else:
                s = cur2["s"]
                S.op("dve", [a.buf, s.buf], [s.buf], lambda e: e.tensor_tensor(out=s.t[:, :TB], in0=a.t[:, :TB], in1=s.t[:, :TB], op=ALU.mult))
                S.op("dve", [s.buf, xT.buf], [xT.buf], lambda e: e.tensor_tensor(out=xT.t[:, fc, :], in0=s.t[:, :TB], in1=xT.t[:, fc, :], op=ALU.add))

        gemm([(plain(W["ple_w_gate"], 0, D, 512), D, rh, [hT.buf]),
              (plain(W["ple_w_proj"], 0, PLE, 512), PLE, lambda kc, c0, c1: pT.t[:, kc, c0:c1], [pT.buf])],
             D // 512, 512, n, epi_p)
        sc5.close()
        rmsnorm_T(xT, n, 4, xT)
        sc6 = Scope()
        xt = sc6.sb("xtok_o", [128, D], F32)
        store_out(blk, xT, xt)
        sc6.close()
        sc0.close()

    nb = NBLK
    for blk in range(nb):
        phase_a(blk)
    exchange()
    for blk in range(nb):
        phase_b(blk)

    S.wait_all("sp", [ybuf])
    for b in dbg_bufs:
        S.wait_all("pool", [b])
    return nc


def _consts(first_of_seq):
    c = np.zeros((128, CW), np.float32)
    c[:, 0:128] = np.eye(128, dtype=np.float32)
    q = np.arange(128)[:, None]
    k = np.arange(256)[None, :]
    dist = q + 128 - k
    allowed = (dist >= 0) & (dist < 128)
    c[:, 128:384] = np.where(allowed, 0.0, -1e9)
    allowedA = allowed & (k >= 128) if first_of_seq else allowed
    c[:, 384:640] = np.where(allowedA, 0.0, -1e9)
    for m in range(128):
        dm = m % 64
        if dm < 8:
            c[m + 8, 640 + m] = -1.0
        elif dm < 16:
            c[m - 8, 640 + m] = 1.0
    t = np.arange(512)
    c[:, 768:1280] = (t % 64 != 0).astype(np.float32)[None, :]
    s = np.arange(128)[:, None]
    tt = np.arange(128)[None, :]
    c[:, 1280:1408] = ((s // 64 == tt // 64) & (s <= tt)).astype(np.float32)
    return c


def _norm_pack(*vs):
    out = np.zeros((128, 6, 16), np.float32)
    for i, v in enumerate(vs):
        out[:, i, :] = np.asarray(v, np.float32).reshape(16, 128).T
    return out


def _small(hgrn_lower_bound, hgrn_norm, attn_sinks, core, partner):
    s = np.zeros((128, 64), np.float32)
    lbp = np.asarray(hgrn_lower_bound, np.float32)
    s[:, 0:8] = lbp[0].reshape(8, 128).T
    s[:, 8:16] = lbp[1].reshape(8, 128).T
    s[:, 16:24] = np.asarray(hgrn_norm, np.float32).reshape(8, 128).T
    s[:, 24:40] = np.asarray(attn_sinks, np.float32).reshape(1, 16)
    if partner is not None:
        s[:, 40 + partner] = 1.0
    inv = np.power(np.float32(500000.0), -np.arange(0, 16, 2, dtype=np.float32) / np.float32(16)).astype(np.float32)
    for m in range(128):
        dm = m % 64
        if dm < 16:
            s[m, 48] = inv[dm % 8]
    return s


def make_in_maps(inputs, ncores=NCORES):
    x = np.asarray(inputs["x"], np.float32)
    B, SEQ, _ = x.shape
    xf = x.reshape(B * SEQ, D)
    pf = np.asarray(inputs["p"], np.float32).reshape(B * SEQ, PLE)
    posf = np.asarray(inputs["positions"]).astype(np.int32).reshape(B * SEQ)
    norms = _norm_pack(inputs["ffn1_norm"][0], inputs["mix_norm"][0], inputs["ffn2_norm"][0], inputs["ple_norm"][0], inputs["final_norm"])
    wts = {k: np.ascontiguousarray(np.asarray(inputs[k], np.float32)[0]) for k in
           ("ffn1_w_gate", "ffn1_w_up", "ffn1_w_down", "w_in", "w_up_a", "w_up_b", "w_out",
            "ffn2_w_gate", "ffn2_w_up", "ffn2_w_down", "ple_w_gate", "ple_w_proj")}
    in_maps = []
    for c in range(ncores):
        r0 = c * TOK
        first = (r0 % SEQ) == 0
        xc = np.zeros((TOK + LB, D), np.float32)
        xc[:TOK] = xf[r0:r0 + TOK]
        pc = np.zeros((1, TOK + LB), np.int32)
        pc[0, :TOK] = posf[r0:r0 + TOK]
        if not first:
            xc[TOK:] = xf[r0 - LB:r0]
            pc[0, TOK:] = posf[r0 - LB:r0]
        m = {"x": xc, "p": np.ascontiguousarray(pf[r0:r0 + TOK]), "pos": pc, "consts": _consts(first), "norms": norms,
             "small": _small(inputs["hgrn_lower_bound"], inputs["hgrn_norm"][0], inputs["attn_sinks"][0], c, None if first else (c - 1) % 2)}
        m.update(wts)
        in_maps.append(m)
    return in_maps, (B, SEQ)


def kernel(**inputs):
    in_maps, (B, SEQ) = make_in_maps(inputs)
    nc = build("full")
    res = run_bass_kernel_spmd(nc, in_maps, core_ids=list(range(NCORES)))
    y = np.concatenate([r["y"] for r in res.results], axis=0)
    return y.reshape(B, SEQ, D).astype(np.float32)
```

```python
from contextlib import ExitStack
import numpy as np
import ml_dtypes
import concourse.bass as bass
import concourse.mybir as mybir
from concourse.bass_utils import run_bass_kernel_spmd

F32 = mybir.dt.float32
BF16 = mybir.dt.bfloat16
I32 = mybir.dt.int32
AF = mybir.ActivationFunctionType
ALU = mybir.AluOpType
AX = mybir.AxisListType

NCORES = 8
D = 2048
FF = 5632
FSPLIT = [(0, 3072), (3072, 5632)]
FH = 3072
TOK = 2048
TB = 512
NBLK = TOK // TB
LB = 128
EPS = 1e-6
IN_DIM = 9728
PLE = 256
WSLOT = 8192
NWSLOT = 4


class Buf:
    __slots__ = ("name", "w", "r", "dsem", "dcnt")

    def __init__(self, name):
        self.name = name
        self.w = {}
        self.r = {}
        self.dsem = None
        self.dcnt = 0


class Sched:
    def __init__(self, nc):
        self.nc = nc
        self.E = {"pe": nc.tensor, "act": nc.scalar, "dve": nc.vector, "pool": nc.gpsimd, "sp": nc.sync}
        self.sem = {e: nc.alloc_semaphore("prog_" + e) for e in ("pe", "act", "dve", "pool")}
        self.cnt = {e: 0 for e in self.sem}
        self.seen = {e: {} for e in self.E}
        self.nsem = 0
        self.sem_pool = []
        self.sem_cnt = {}

    def _wait(self, e, key, tok):
        kind, s, v = tok
        if kind == "e" and s == e and e == "pe":
            return
        if self.seen[e].get(key, 0) >= v:
            return
        sem = self.sem[s] if kind == "e" else s
        self.E[e].wait_ge(sem, v)
        self.seen[e][key] = v

    def deps(self, e, reads, writes):
        for b in reads:
            for k, t in b.w.items():
                self._wait(e, k, t)
        for b in writes:
            for k, t in b.w.items():
                self._wait(e, k, t)
            for k, t in b.r.items():
                self._wait(e, k, t)

    @staticmethod
    def _rec(d, key, tok):
        old = d.get(key)
        if old is None or old[2] < tok[2]:
            d[key] = tok

    def fin(self, e, inst, reads, writes, sig=True):
        if sig:
            self.cnt[e] += 1
            inst.then_inc(self.sem[e], 1)
            idx = self.cnt[e]
        else:
            idx = self.cnt[e] + 1
        tok = ("e", e, idx)
        for b in reads:
            self._rec(b.r, e, tok)
        for b in writes:
            self._rec(b.w, e, tok)

    def op(self, e, reads, writes, mk):
        self.deps(e, reads, writes)
        inst = mk(self.E[e])
        self.fin(e, inst, reads, writes, True)

    def mm(self, out, lhsT, rhs, start, stop, reads, writes, sig, skip=False):
        self.deps("pe", reads, writes)
        inst = self.nc.tensor.matmul(out, lhsT, rhs, start=start, stop=stop, skip_group_check=skip)
        self.fin("pe", inst, reads, writes, sig)

    def tr(self, out, in_, ident, reads, writes, sig):
        self.deps("pe", reads, writes)
        inst = self.nc.tensor.transpose(out, in_, ident)
        self.fin("pe", inst, reads, writes, sig)

    def dma(self, q, out, in_, reads, writes, sembuf):
        self.deps(q, reads, writes)
        if sembuf.dsem is None:
            if self.sem_pool:
                sembuf.dsem = self.sem_pool.pop()
            else:
                self.nsem += 1
                sembuf.dsem = (self.nc.alloc_semaphore("dsem%d" % self.nsem), self.nsem)
                self.sem_cnt[self.nsem] = 0
        sem, sid = sembuf.dsem
        self.sem_cnt[sid] += 1
        self.E[q].dma_start(out=out, in_=in_).then_inc(sem, 16)
        tok = ("d", sem, 16 * self.sem_cnt[sid])
        key = "dsem%d" % sid
        for b in reads:
            self._rec(b.r, key, tok)
        for b in writes:
            self._rec(b.w, key, tok)

    def wait_all(self, e, bufs):
        self.deps(e, [], bufs)


class T:
    def __init__(self, t, name):
        self.t = t
        self.buf = Buf(name)


class Ring:
    def __init__(self, items):
        self.items = items
        self.i = 0

    def next(self):
        it = self.items[self.i % len(self.items)]
        self.i += 1
        return it


W_OFF = dict(qa=0, ka=1024, va=1280, qb=1536, fb=2560, ib=3584, og=4608, ga=5632, gb=7680)
CW = 1408
TWO_PI = 6.283185307179586


def build(mode="full", ncores=NCORES):
    nc = bass.Bass("TRN2", target_bir_lowering=False)
    S = Sched(nc)
    uid = [0]

    def din(name, shape, dt=F32):
        return nc.dram_tensor(name, list(shape), dt, kind="ExternalInput").ap()

    def dscr(name, shape, dt=F32):
        kind = "ExternalOutput" if (mode.startswith("dbg") and name not in ("st_loc", "st_all")) else "Internal"
        return nc.dram_tensor(name, list(shape), dt, kind=kind).ap()

    x_in = din("x", [TOK + LB, D])
    p_in = din("p", [TOK, PLE])
    pos_in = din("pos", [1, TOK + LB], I32)
    cst_in = din("consts", [128, CW])
    nrm_in = din("norms", [128, 6, 16])
    sml_in = din("small", [128, 64])
    W = {k: din(k, shp) for k, shp in [
        ("ffn1_w_gate", [D, FF]), ("ffn1_w_up", [D, FF]), ("ffn1_w_down", [FF, D]), ("w_in", [D, IN_DIM]),
        ("w_up_a", [1024, D]), ("w_up_b", [1024, D]), ("w_out", [D, D]),
        ("ffn2_w_gate", [D, FF]), ("ffn2_w_up", [D, FF]), ("ffn2_w_down", [FF, D]),
        ("ple_w_gate", [D, D]), ("ple_w_proj", [PLE, D])]}
    w_in = W["w_in"]
    y_out = nc.dram_tensor("y", [TOK, D], F32, kind="ExternalOutput").ap()
    x1s = dscr("x1s", [128, 16, TOK]); x1s_b = [Buf("x1s%d" % i) for i in range(NBLK)]
    oas = dscr("oas", [128, 8, TOK], BF16); oas_b = [Buf("oas%d" % i) for i in range(NBLK)]
    obs = dscr("obs", [128, 8, TOK]); obs_b = [Buf("obs%d" % i) for i in range(NBLK)]
    qts = dscr("qts", [128, 8, TOK], BF16); qts_b = [Buf("qts%d" % i) for i in range(NBLK)]
    st_loc = dscr("st_loc", [128, 1024]); st_loc_b = Buf("st_loc")
    st_all = dscr("st_all", [2 * 128, 1024]); st_all_b = Buf("st_all")
    ybuf = Buf("ydram")
    dbg_bufs = []

    def mk(alloc, name, shape, dt):
        uid[0] += 1
        nm = "%s_%d" % (name, uid[0])
        return T(alloc(nm, list(shape), dt), nm)

    def sb(name, shape, dt):
        return mk(nc.alloc_sbuf_tensor, name, shape, dt)

    def ps(name, shape, dt=F32):
        return mk(nc.alloc_psum_tensor, name, shape, dt)

    class Scope:
        def __init__(self):
            self.st = ExitStack()
            self.items = []

        def sb(self, name, shape, dt):
            uid[0] += 1
            nm = "%s_%d" % (name, uid[0])
            t = T(self.st.enter_context(nc.sbuf_tensor(nm, list(shape), dt)), nm)
            self.items.append(t)
            return t

        def close(self):
            barrier(self.items)
            for t in self.items:
                if t.buf.dsem is not None:
                    S.sem_pool.append(t.buf.dsem)
                    t.buf.dsem = None
            self.st.close()

    cst = sb("cst", [128, CW], F32)
    nrm = sb("nrm", [128, 6, 16], F32)
    sml = sb("sml", [128, 64], F32)
    ident_f = cst.t[:, 0:128]
    maskB = cst.t[:, 128:384]
    maskA = cst.t[:, 384:640]
    prot = cst.t[:, 640:768]
    resetm = cst.t[:, 768:1280]
    bdmask = cst.t[:, 1280:1408]
    ident_b = sb("ident_b", [128, 128], BF16)
    ones_b = sb("ones_b", [128, 128], BF16)
    cc = sb("cc", [128, 64], F32)
    epsc = cc.t[:, 0:1]; pic = cc.t[:, 1:2]
    lbc = cc.t[:, 8:16]; omlb = cc.t[:, 16:24]; nomlb = cc.t[:, 24:32]; negsink = cc.t[:, 32:48]
    hgw = sml.t[:, 16:24]; sink = sml.t[:, 24:40]; sel = sml.t[:, 40:48]; invf = sml.t[:, 48:49]
    zeros_f = sb("zeros_f", [128, 512], F32)
    bar = sb("bar", [128, 1], F32)
    wslots = Ring([sb("wslot%d" % i, [128, WSLOT], BF16) for i in range(NWSLOT)])
    hT = sb("hT", [128, 16, TB + LB], BF16)
    sq = Ring([sb("sq%d" % i, [128, TB + LB], BF16) for i in range(2)])
    sg = Ring([sb("sg%d" % i, [128, TB + LB], F32) for i in range(2)])
    rstd = sb("rstd", [128, TB + LB], F32)
    kT = sb("kT", [128, 4, LB + TB], BF16)
    Vpad = sb("Vpad", [128, 5, 4, 2, 128], BF16)
    S_f = sb("S_f", [128, 8, 128], F32)
    S_bf = sb("S_bf", [128, 8, 128], BF16)
    carry = sb("carry", [128, 8], F32)
    accs = Ring([ps("acc%d" % i, [128, 1024]) for i in range(2)])
    aux = Ring([ps("aux%d" % i, [128, 512]) for i in range(3)])
    auxb = ps("auxb", [128, 1024], BF16)

    def barrier(items):
        for e in ("pe", "act", "pool"):
            if S.cnt[e] > 0:
                S._wait("dve", e, ("e", e, S.cnt[e]))
        for t in items:
            S.deps("dve", [], [t.buf])
        S.op("dve", [], [bar.buf], lambda e: e.memset(bar.t[:], 0.0))
        m = S.cnt["dve"]
        for e in ("act", "sp"):
            S._wait(e, "dve", ("e", "dve", m))

    def dump(name, src):
        shp = list(src.t.shape)
        o = nc.dram_tensor("dbg_" + name, shp, F32, kind="ExternalOutput").ap()
        b = Buf("dbg_" + name)
        idx = tuple(slice(None) for _ in shp)
        S.dma("pool", o[idx], src.t[idx], [src.buf], [b], b)
        dbg_bufs.append(b)

    S.dma("sp", cst.t[:], cst_in[:, :], [], [cst.buf], cst.buf)
    S.dma("sp", nrm.t[:], nrm_in[:, :, :], [], [nrm.buf], nrm.buf)
    S.dma("sp", sml.t[:], sml_in[:, :], [], [sml.buf], sml.buf)
    S.op("dve", [cst.buf], [ident_b.buf], lambda e: e.tensor_copy(ident_b.t[:], ident_f))
    S.op("dve", [], [ones_b.buf], lambda e: e.memset(ones_b.t[:], 1.0))
    S.op("dve", [], [cc.buf], lambda e: e.memset(cc.t[:, 0:1], EPS))
    S.op("dve", [], [cc.buf], lambda e: e.memset(cc.t[:, 1:2], float(np.pi)))
    S.op("dve", [], [zeros_f.buf], lambda e: e.memset(zeros_f.t[:], 0.0))
    S.op("dve", [], [Vpad.buf], lambda e: e.memset(Vpad.t[:], 0.0))
    S.op("dve", [], [S_f.buf], lambda e: e.memset(S_f.t[:], 0.0))
    S.op("dve", [], [S_bf.buf], lambda e: e.memset(S_bf.t[:], 0.0))
    S.op("dve", [], [carry.buf], lambda e: e.memset(carry.t[:], 0.0))
    S.op("dve", [sml.buf], [cc.buf], lambda e: e.tensor_tensor(out=lbc, in0=sml.t[:, 0:8], in1=sml.t[:, 8:16], op=ALU.subtract))
    S.op("act", [cc.buf], [cc.buf], lambda e: e.activation(out=lbc, in_=lbc, func=AF.Sigmoid))
    S.op("dve", [cc.buf], [cc.buf], lambda e: e.tensor_scalar(out=omlb, in0=lbc, scalar1=-1.0, scalar2=1.0, op0=ALU.mult, op1=ALU.add))
    S.op("dve", [cc.buf], [cc.buf], lambda e: e.tensor_scalar(out=nomlb, in0=lbc, scalar1=-1.0, scalar2=None, op0=ALU.add))
    S.op("dve", [sml.buf], [cc.buf], lambda e: e.tensor_scalar(out=negsink, in0=sink, scalar1=-1.0, scalar2=None, op0=ALU.mult))

    def splits_of(n):
        return [(0, 512)] + ([(512, n)] if n > 512 else [])

    def transpose_in(src_dram_rows, ncol_chunks, xt, dst, c0, dst_is_bf16=False):
        S.dma("sp", xt.t[:, 0:ncol_chunks * 128], src_dram_rows, [], [xt.buf], xt.buf)
        for g in range((ncol_chunks + 3) // 4):
            a = aux.next()
            k = min(4, ncol_chunks - g * 4)
            for i in range(k):
                dc = g * 4 + i
                S.tr(a.t[:, i * 128:(i + 1) * 128], xt.t[:, dc * 128:(dc + 1) * 128], ident_f,
                     [xt.buf, cst.buf], [a.buf], sig=(i == k - 1))
            src = a.t[:, 0:k * 128].rearrange("p (i t) -> p i t", i=k)
            d = dst.t[:, g * 4:g * 4 + k, c0:c0 + 128]
            if g % 2 == 0:
                S.op("act", [a.buf], [dst.buf], lambda e: e.activation(out=d, in_=src, func=AF.Copy))
            else:
                S.op("dve", [a.buf], [dst.buf], lambda e: e.tensor_copy(d, src))

    def rmsnorm_T(src, n, widx, dst):
        spl = splits_of(n)
        a = accs.next()
        for dc in range(16):
            s = sq.next()
            S.op("act", [src.buf], [s.buf], lambda e: e.activation(out=s.t[:, :n], in_=src.t[:, dc, :n], func=AF.Square))
            for (c0, c1) in spl:
                S.mm(a.t[:, c0:c1], ones_b.t[:], s.t[:, c0:c1], dc == 0, dc == 15, [ones_b.buf, s.buf], [a.buf], sig=True)
        S.op("act", [a.buf, cc.buf], [rstd.buf],
             lambda e: e.activation(out=rstd.t[:, :n], in_=a.t[:, :n], func=AF.Sqrt, scale=1.0 / D, bias=epsc))
        S.op("dve", [rstd.buf], [rstd.buf], lambda e: e.reciprocal(rstd.t[:, :n], rstd.t[:, :n]))
        for dc in range(16):
            S.op("dve", [src.buf, rstd.buf, nrm.buf], [dst.buf],
                 lambda e: e.scalar_tensor_tensor(out=dst.t[:, dc, :n], in0=src.t[:, dc, :n],
                                                  scalar=nrm.t[:, widx, dc:dc + 1], in1=rstd.t[:, :n],
                                                  op0=ALU.mult, op1=ALU.mult))

    def plain(w, col0, K, fgroup):
        KC = K // 128

        def load(g, sl):
            c = col0 + g * fgroup
            S.dma("pool", sl.t[:, 0:KC * fgroup].rearrange("p (kc f) -> p kc f", f=fgroup),
                  w[:, c:c + fgroup].rearrange("(kc p) f -> p kc f", p=128), [], [sl.buf], sl.buf)
        return load

    def gemm(srcs, ngroups, fgroup, n, epi):
        spl = splits_of(n)
        for g in range(ngroups):
            slots = []
            for (load, K, rhs, rbufs) in srcs:
                assert (K // 128) * fgroup <= WSLOT
                sl = wslots.next()
                load(g, sl)
                slots.append(sl)
            for fc in range(fgroup // 128):
                for wi, sl in enumerate(slots):
                    (load, K, rhs, rbufs) = srcs[wi]
                    KC = K // 128
                    a = accs.next()
                    for (c0, c1) in spl:
                        for kc in range(KC):
                            o = kc * fgroup + fc * 128
                            S.mm(a.t[:, c0:c1], sl.t[:, o:o + 128], rhs(kc, c0, c1), kc == 0, kc == KC - 1,
                                 [sl.buf] + rbufs, [a.buf], sig=(kc == KC - 1))
                    epi(wi, g * (fgroup // 128) + fc, a)

    def gemm_tm(w, col0, K, cols, cgroup, lhs, lbufs, tiles, epi):
        KC = K // 128
        for g in range(cols // cgroup):
            sl = wslots.next()
            plain(w, col0, K, cgroup)(g, sl)
            for tl in tiles:
                a = accs.next()
                for kc in range(KC):
                    S.mm(a.t[:, 0:cgroup], lhs(kc, tl), sl.t[:, kc * cgroup:(kc + 1) * cgroup], kc == 0, kc == KC - 1,
                         [sl.buf] + lbufs, [a.buf], sig=(kc == KC - 1))
                epi(tl, g, a)

    def ffn(xT, actT, n, widx, wg, wu, wd):
        rmsnorm_T(xT, n, widx, hT)
        for (fa, fb) in FSPLIT:
            cur = {}

            def epi_gu(wi, fc, a):
                if wi == 0:
                    s = sg.next()
                    cur["s"] = s
                    S.op("act", [a.buf], [s.buf], lambda e: e.activation(out=s.t[:, :n], in_=a.t[:, :n], func=AF.Silu))
                else:
                    s = cur["s"]
                    S.op("dve", [a.buf, s.buf], [actT.buf],
                         lambda e: e.tensor_tensor(out=actT.t[:, fc, :n], in0=a.t[:, :n], in1=s.t[:, :n], op=ALU.mult))

            rh = (lambda kc, c0, c1: hT.t[:, kc, c0:c1])
            gemm([(plain(wg, fa, D, 512), D, rh, [hT.buf]), (plain(wu, fa, D, 512), D, rh, [hT.buf])],
                 (fb - fa) // 512, 512, n, epi_gu)

            def epi_d(wi, fc, a):
                S.op("dve", [a.buf, xT.buf], [xT.buf],
                     lambda e: e.scalar_tensor_tensor(out=xT.t[:, fc, :n], in0=a.t[:, :n], scalar=0.5,
                                                      in1=xT.t[:, fc, :n], op0=ALU.mult, op1=ALU.add))

            gemm([(plain(wd[fa:fb, :], 0, fb - fa, 256), fb - fa, lambda kc, c0, c1: actT.t[:, kc, c0:c1], [actT.buf])],
                 D // 256, 256, n, epi_d)

    def store_out(blk, src, xt):
        for t in range(TB // 128):
            for g in range(4):
                a = aux.next()
                for i in range(4):
                    dc = g * 4 + i
                    S.tr(a.t[:, i * 128:(i + 1) * 128], src.t[:, dc, t * 128:(t + 1) * 128], ident_f,
                         [src.buf, cst.buf], [a.buf], sig=(i == 3))
                dst = xt.t[:, g * 512:(g + 1) * 512]
                if g % 2 == 0:
                    S.op("act", [a.buf], [xt.buf], lambda e: e.activation(out=dst, in_=a.t[:, :], func=AF.Copy))
                else:
                    S.op("dve", [a.buf], [xt.buf], lambda e: e.tensor_copy(dst, a.t[:, :]))
            r0 = blk * TB + t * 128
            S.dma("sp", y_out[r0:r0 + 128, :], xt.t[:], [xt.buf], [ybuf], xt.buf)

    def phase_a(blk):
        n = TB + LB if blk == 0 else TB
        sc = Scope()
        xT = sc.sb("xT", [128, 16, TB + LB], F32)
        xt = sc.sb("xtok", [128, D], F32)
        actT = sc.sb("actT", [128, FH // 128, TB + LB], BF16)
        tiles = [(blk * TB + t * 128, t * 128) for t in range(4)]
        if blk == 0:
            tiles.append((TOK, TB))
        for (r0, c0) in tiles:
            transpose_in(x_in[r0:r0 + 128, :], 16, xt, xT, c0)
        ffn(xT, actT, n, 0, W["ffn1_w_gate"], W["ffn1_w_up"], W["ffn1_w_down"])
        rmsnorm_T(xT, n, 1, hT)
        S.dma("sp", x1s[:, :, blk * TB:(blk + 1) * TB], xT.t[:, :, 0:TB], [xT.buf], [x1s_b[blk]], xT.buf)
        if mode == "dbgA" and blk == 0:
            dump("x1T", xT)
            dump("hT", hT)
        sc.close()
        if mode == "ffn_only":
            return
        mixer_a(blk, n)

    def mixer_a(blk, n):
        sc = Scope()
        posi = sc.sb("posi", [128, TB + LB], I32)
        cosT = sc.sb("cosT", [128, TB + LB], F32)
        sinT = sc.sb("sinT", [128, TB + LB], F32)
        S.dma("sp", posi.t[:, 0:TB], pos_in[0:1, blk * TB:(blk + 1) * TB].broadcast_to([128, TB]), [], [posi.buf], posi.buf)
        if blk == 0:
            S.dma("sp", posi.t[:, TB:n], pos_in[0:1, TOK:TOK + LB].broadcast_to([128, LB]), [], [posi.buf], posi.buf)
        S.op("dve", [posi.buf], [sinT.buf], lambda e: e.tensor_copy(sinT.t[:, :n], posi.t[:, :n]))
        S.op("dve", [sinT.buf, sml.buf], [sinT.buf], lambda e: e.tensor_scalar(out=sinT.t[:, :n], in0=sinT.t[:, :n], scalar1=invf, scalar2=None, op0=ALU.mult))
        S.op("dve", [sinT.buf], [cosT.buf], lambda e: e.tensor_scalar(out=cosT.t[:, :n], in0=sinT.t[:, :n], scalar1=float(1.0 / TWO_PI), scalar2=12582912.0, op0=ALU.mult, op1=ALU.add))
        S.op("dve", [cosT.buf], [cosT.buf], lambda e: e.tensor_scalar(out=cosT.t[:, :n], in0=cosT.t[:, :n], scalar1=-12582912.0, scalar2=None, op0=ALU.add))
        S.op("dve", [cosT.buf, sinT.buf], [sinT.buf], lambda e: e.scalar_tensor_tensor(out=sinT.t[:, :n], in0=cosT.t[:, :n], scalar=-6.28125, in1=sinT.t[:, :n], op0=ALU.mult, op1=ALU.add))
        S.op("dve", [cosT.buf, sinT.buf], [sinT.buf], lambda e: e.scalar_tensor_tensor(out=sinT.t[:, :n], in0=cosT.t[:, :n], scalar=-(TWO_PI - 6.28125), in1=sinT.t[:, :n], op0=ALU.mult, op1=ALU.add))
        S.op("dve", [sinT.buf], [sinT.buf], lambda e: e.tensor_scalar(out=sinT.t[:, :n], in0=sinT.t[:, :n], scalar1=-3.1415925, scalar2=3.1415925, op0=ALU.max, op1=ALU.min))
        S.op("act", [sinT.buf], [cosT.buf], lambda e: e.activation(out=cosT.t[:, :n], in_=sinT.t[:, :n], func=AF.Sin, scale=0.5))
        S.op("act", [sinT.buf], [sinT.buf], lambda e: e.activation(out=sinT.t[:, :n], in_=sinT.t[:, :n], func=AF.Sin))
        S.op("dve", [cosT.buf], [cosT.buf], lambda e: e.tensor_tensor(out=cosT.t[:, :n], in0=cosT.t[:, :n], in1=cosT.t[:, :n], op=ALU.mult))
        S.op("dve", [cosT.buf], [cosT.buf], lambda e: e.tensor_scalar(out=cosT.t[:, :n], in0=cosT.t[:, :n], scalar1=-2.0, scalar2=1.0, op0=ALU.mult, op1=ALU.add))
        qf = Ring([sc.sb("qf%d" % i, [128, TB + LB], F32) for i in range(2)])
        rt = Ring([sc.sb("rt%d" % i, [128, TB + LB], F32) for i in range(2)])

        def rope(a, nn, writes):
            f = qf.next(); r = rt.next()
            S.op("act", [a.buf], [f.buf], lambda e: e.activation(out=f.t[:, :nn], in_=a.t[:, :nn], func=AF.Copy))
            a2 = accs.next()
            for (c0, c1) in splits_of(nn):
                S.mm(a2.t[:, c0:c1], prot, f.t[:, c0:c1], True, True, [cst.buf, f.buf], [a2.buf], sig=True)
            S.op("dve", [a2.buf, sinT.buf], [r.buf], lambda e: e.tensor_tensor(out=r.t[:, :nn], in0=a2.t[:, :nn], in1=sinT.t[:, :nn], op=ALU.mult))
            S.op("dve", [f.buf, cosT.buf], [f.buf], lambda e: e.tensor_tensor(out=f.t[:, :nn], in0=f.t[:, :nn], in1=cosT.t[:, :nn], op=ALU.mult))
            return f, r

        rh = (lambda kc, c0, c1: hT.t[:, kc, c0:c1])
        def load_k(g, sl):
            for half in range(2):
                c = W_OFF["ka"] + g * 64
                S.dma("pool", sl.t[:, 0:16 * 128].rearrange("p (kc f) -> p kc f", f=128)[:, :, half * 64:(half + 1) * 64],
                      w_in[:, c:c + 64].rearrange("(kc p) f -> p kc f", p=128), [], [sl.buf], sl.buf)

        def epi_k(wi, g, a):
            f, r = rope(a, n, None)
            S.op("dve", [f.buf, r.buf], [kT.buf], lambda e: e.tensor_tensor(out=kT.t[:, g, LB:LB + TB], in0=f.t[:, 0:TB], in1=r.t[:, 0:TB], op=ALU.add))
            if n > TB:
                S.op("dve", [f.buf, r.buf], [kT.buf], lambda e: e.tensor_tensor(out=kT.t[:, g, 0:LB], in0=f.t[:, TB:n], in1=r.t[:, TB:n], op=ALU.add))

        gemm([(load_k, D, rh, [hT.buf])], 4, 128, n, epi_k)

        vtiles = [0, 1, 2, 3] + ([4] if blk == 0 else [])

        def epi_v(tl, g, a):
            vt = 0 if tl == 4 else tl + 1
            src = a.t[:, 0:256].rearrange("p (g d) -> p g d", g=4)
            S.op("act", [a.buf], [Vpad.buf], lambda e: e.activation(out=Vpad.t[:, vt, :, 0, 0:64], in_=src, func=AF.Copy))
            S.op("dve", [a.buf], [Vpad.buf], lambda e: e.tensor_copy(Vpad.t[:, vt, :, 1, 64:128], src))

        gemm_tm(w_in, W_OFF["va"], D, 256, 256, lambda kc, tl: hT.t[:, kc, tl * 128:(tl + 1) * 128], [hT.buf], vtiles, epi_v)

        qrot = [sc.sb("qrot%d" % i, [128, TB], BF16) for i in range(8)]
        oaT = [sc.sb("oaT%d" % i, [128, TB], BF16) for i in range(8)]
        ssb = Ring([sc.sb("ssb%d" % i, [128, 256], F32) for i in range(3)])
        eb = Ring([sc.sb("eb%d" % i, [128, 256], F32) for i in range(3)])
        pb = Ring([sc.sb("pb%d" % i, [128, 256], BF16) for i in range(4)])
        pt = Ring([sc.sb("pt%d" % i, [128, 256], BF16) for i in range(4)])
        sm = Ring([sc.sb("sm%d" % i, [128, 8], F32) for i in range(6)])

        def epi_q(wi, fc, a):
            f, r = rope(a, TB, None)
            q = qrot[fc]
            S.op("dve", [f.buf, r.buf], [q.buf], lambda e: e.tensor_tensor(out=q.t[:, :], in0=f.t[:, 0:TB], in1=r.t[:, 0:TB], op=ALU.add))

        gemm([(plain(w_in, W_OFF["qa"], D, 512), D, rh, [hT.buf])], 2, 512, TB, epi_q)

        class View:
            def __init__(self, t, buf):
                self.t = t
                self.buf = buf

        sc_slots = Ring([View(aux.items[b].t[:, 0:256], aux.items[b].buf) for b in range(2)])
        oa_slots = Ring([View(aux.items[2].t[:, 0:128], aux.items[2].buf)])
        tb_slots = Ring([View(auxb.t[:, 0:256], auxb.buf)])
        units = [(g, j, pair, hh) for g in range(4) for j in range(4) for pair in range(2) for hh in range(2)]
        st = {}

        def stA(u):
            g, j, pair, hh = units[u]
            h = 4 * g + 2 * pair + hh
            base = hh * 64
            q = qrot[2 * g + pair]
            msk = maskA if (blk == 0 and j == 0) else maskB
            a = sc_slots.next()
            S.mm(a.t, q.t[base:base + 64, j * 128:(j + 1) * 128], kT.t[base:base + 64, g, j * 128:j * 128 + 256],
                 True, True, [q.buf, kT.buf], [a.buf], sig=True)
            s_ = ssb.next(); e_ = eb.next(); p_ = pb.next(); m_ = sm.next()
            S.op("dve", [a.buf], [m_.buf], lambda e: e.reduce_max(out=m_.t[:, 0:1], in_=a.t, axis=AX.X))
            S.op("dve", [m_.buf, cc.buf], [m_.buf], lambda e: e.tensor_scalar(out=m_.t[:, 1:2], in0=m_.t[:, 0:1], scalar1=-0.125, scalar2=negsink[:, h:h + 1], op0=ALU.mult, op1=ALU.min))
            S.op("act", [a.buf, m_.buf], [e_.buf], lambda e: e.activation(out=e_.t[:, :], in_=a.t, func=AF.Exp, scale=0.125, bias=m_.t[:, 1:2]))
            S.op("act", [m_.buf, sml.buf], [m_.buf], lambda e: e.activation(out=m_.t[:, 3:4], in_=m_.t[:, 1:2], func=AF.Exp, scale=1.0, bias=sink[:, h:h + 1]))
            S.op("dve", [e_.buf, cst.buf], [s_.buf, m_.buf], lambda e: e.scalar_tensor_tensor(out=s_.t[:, :], in0=e_.t[:, :], scalar=1.0, in1=msk, op0=ALU.mult, op1=ALU.mult, accum_out=m_.t[:, 2:3]))
            S.op("dve", [m_.buf], [m_.buf], lambda e: e.tensor_tensor(out=m_.t[:, 4:5], in0=m_.t[:, 2:3], in1=m_.t[:, 3:4], op=ALU.add))
            S.op("dve", [m_.buf], [m_.buf], lambda e: e.reciprocal(m_.t[:, 5:6], m_.t[:, 4:5]))
            S.op("dve", [s_.buf, m_.buf], [p_.buf], lambda e: e.tensor_scalar(out=p_.t[:, :], in0=s_.t[:, :], scalar1=m_.t[:, 5:6], scalar2=None, op0=ALU.mult))
            st[u] = [p_]

        def stB(u):
            p_ = st[u][0]
            tb = tb_slots.next(); t_ = pt.next()
            S.tr(tb.t[:, 0:128], p_.t[:, 0:128], ident_b.t[:], [p_.buf, ident_b.buf], [tb.buf], sig=False)
            S.tr(tb.t[:, 128:256], p_.t[:, 128:256], ident_b.t[:], [p_.buf, ident_b.buf], [tb.buf], sig=True)
            S.op("act", [tb.buf], [t_.buf], lambda e: e.activation(out=t_.t[:, :], in_=tb.t, func=AF.Copy))
            st[u] = [t_]

        def stC(u):
            g, j, pair, hh = units[u]
            t_ = st.pop(u)[0]
            if hh == 0:
                st["oacc"] = oa_slots.next()
            oacc = st["oacc"]
            for kb in range(2):
                S.mm(oacc.t, Vpad.t[:, j + kb, g, hh, :], t_.t[:, kb * 128:(kb + 1) * 128],
                     (hh == 0 and kb == 0), (hh == 1 and kb == 1), [Vpad.buf, t_.buf], [oacc.buf], sig=(kb == 1))
            if hh == 1:
                o = oaT[2 * g + pair]
                S.op("act", [oacc.buf], [o.buf], lambda e: e.activation(out=o.t[:, j * 128:(j + 1) * 128], in_=oacc.t, func=AF.Copy))
                if j == 3:
                    S.dma("sp", oas[:, 2 * g + pair, blk * TB:(blk + 1) * TB], o.t[:, :], [o.buf], [oas_b[blk]], o.buf)

        NU = len(units); SK1 = 2; SK2 = 1
        for i in range(NU + SK1 + SK2):
            if i < NU:
                stA(i)
            if 0 <= i - SK1 < NU:
                stB(i - SK1)
            if 0 <= i - SK1 - SK2 < NU:
                stC(i - SK1 - SK2)
        S.op("dve", [kT.buf], [kT.buf], lambda e: e.tensor_copy(kT.t[:, :, 0:LB], kT.t[:, :, TB:TB + LB]))
        S.op("dve", [Vpad.buf], [Vpad.buf], lambda e: e.tensor_copy(Vpad.t[:, 0], Vpad.t[:, 4]))
        if mode == "dbgA" and blk == 0:
            dump("kT", kT)
        sc.close()
        if mode == "attn_only":
            return
        sc = Scope()

        vb = sc.sb("vb", [128, 4, 1024], BF16)

        def epi_ib(tl, g, a):
            d = vb.t[:, tl, g * 512:(g + 1) * 512]
            if tl % 2 == 0:
                S.op("act", [a.buf], [vb.buf], lambda e: e.activation(out=d, in_=a.t[:, 0:512], func=AF.Copy))
            else:
                S.op("dve", [a.buf], [vb.buf], lambda e: e.tensor_copy(d, a.t[:, 0:512]))

        gemm_tm(w_in, W_OFF["ib"], D, 1024, 512, lambda kc, tl: hT.t[:, kc, tl * 128:(tl + 1) * 128], [hT.buf], [0, 1, 2, 3], epi_ib)

        qtT = sc.sb("qtT", [128, 8, TB], BF16)
        ktT = sc.sb("ktT", [128, 8, TB], BF16)
        elast = sc.sb("elast", [128, 8, 8], F32)
        qs = Ring([sc.sb("qs%d" % i, [128, TB], F32) for i in range(2)])
        tsg = Ring([sc.sb("tsg%d" % i, [128, TB], F32) for i in range(2)])
        tfg = Ring([sc.sb("tfg%d" % i, [128, TB], F32) for i in range(2)])
        tcu = Ring([sc.sb("tcu%d" % i, [128, TB], F32) for i in range(2)])
        tct = Ring([sc.sb("tct%d" % i, [128, TB], F32) for i in range(1)])
        tex = Ring([sc.sb("tex%d" % i, [128, TB], F32) for i in range(2)])
        qto = Ring([sc.sb("qto%d" % i, [128, TB], BF16) for i in range(2)])
        cur = {}

        def epi_h(wi, hd, a):
            if wi == 0:
                q_ = qs.next(); cur["q"] = q_
                S.op("act", [a.buf], [q_.buf], lambda e: e.activation(out=q_.t[:, :], in_=a.t[:, 0:TB], func=AF.Silu))
                return
            q_ = cur["q"]
            sg_ = tsg.next(); fg_ = tfg.next(); cu_ = tcu.next(); ct_ = tct.next()
            S.op("act", [a.buf], [sg_.buf], lambda e: e.activation(out=sg_.t[:, :], in_=a.t[:, 0:TB], func=AF.Sigmoid))
            S.op("dve", [sg_.buf, cc.buf], [fg_.buf], lambda e: e.tensor_scalar(out=fg_.t[:, :], in0=sg_.t[:, :], scalar1=omlb[:, hd:hd + 1], scalar2=lbc[:, hd:hd + 1], op0=ALU.mult, op1=ALU.add))
            S.op("act", [fg_.buf], [fg_.buf], lambda e: e.activation(out=fg_.t[:, :], in_=fg_.t[:, :], func=AF.Ln))
            S.op("dve", [sg_.buf, cc.buf], [sg_.buf], lambda e: e.tensor_scalar(out=sg_.t[:, :], in0=sg_.t[:, :], scalar1=nomlb[:, hd:hd + 1], scalar2=omlb[:, hd:hd + 1], op0=ALU.mult, op1=ALU.add))
            S.op("dve", [fg_.buf, cst.buf], [cu_.buf], lambda e: e.tensor_tensor_scan(out=cu_.t[:, :], data0=resetm, data1=fg_.t[:, :], initial=0.0, op0=ALU.mult, op1=ALU.add))
            S.op("dve", [fg_.buf, zeros_f.buf, carry.buf], [ct_.buf], lambda e: e.tensor_tensor_scan(out=ct_.t[:, :], data0=zeros_f.t[:, :], data1=fg_.t[:, :], initial=carry.t[:, hd:hd + 1], op0=ALU.add, op1=ALU.add))
            S.op("dve", [ct_.buf], [carry.buf], lambda e: e.tensor_copy(carry.t[:, hd:hd + 1], ct_.t[:, TB - 1:TB]))
            x1 = tex.next()
            S.op("act", [cu_.buf], [x1.buf], lambda e: e.activation(out=x1.t[:, :], in_=cu_.t[:, :], func=AF.Exp))
            S.op("dve", [q_.buf, x1.buf], [qtT.buf], lambda e: e.tensor_tensor(out=qtT.t[:, hd, :], in0=q_.t[:, :], in1=x1.t[:, :], op=ALU.mult))
            x2 = tex.next()
            S.op("act", [cu_.buf], [x2.buf], lambda e: e.activation(out=x2.t[:, :], in_=cu_.t[:, :], func=AF.Exp, scale=-1.0))
            S.op("dve", [sg_.buf, x2.buf], [ktT.buf], lambda e: e.tensor_tensor(out=ktT.t[:, hd, :], in0=sg_.t[:, :], in1=x2.t[:, :], op=ALU.mult))
            x3 = tex.next(); qo = qto.next()
            S.op("act", [ct_.buf], [x3.buf], lambda e: e.activation(out=x3.t[:, :], in_=ct_.t[:, :], func=AF.Exp))
            S.op("dve", [q_.buf, x3.buf], [qo.buf], lambda e: e.tensor_tensor(out=qo.t[:, :], in0=q_.t[:, :], in1=x3.t[:, :], op=ALU.mult))
            S.dma("sp", qts[:, hd, blk * TB:(blk + 1) * TB], qo.t[:, :], [qo.buf], [qts_b[blk]], qo.buf)
            S.op("act", [cu_.buf], [elast.buf], lambda e: e.activation(out=elast.t[:, hd, :], in_=cu_.t[:, :].rearrange("p (c t) -> p c t", t=64)[:, :, 63], func=AF.Exp))

        gemm([(plain(w_in, W_OFF["qb"], D, 512), D, rh, [hT.buf]), (plain(w_in, W_OFF["fb"], D, 512), D, rh, [hT.buf])], 2, 512, TB, epi_h)
        if mode == "dbgA" and blk == 0:
            dump("qtT", qtT); dump("ktT", ktT); dump("elast", elast)

        ktok = Ring([sc.sb("ktok%d" % i, [128, 128], BF16) for i in range(8)])
        AT = Ring([sc.sb("AT%d" % i, [128, 128], BF16) for i in range(8)])
        stmp = sc.sb("stmp", [128, 8, 128], F32)
        ol = Ring([sc.sb("ol%d" % i, [128, 8, 128], F32) for i in range(2)])
        for pr in range(4):
            oac = accs.next()
            su = accs.next()
            kts = []
            c0 = pr * 128
            ats = []
            for hd in range(8):
                S.tr(auxb.t[:, hd * 128:(hd + 1) * 128], ktT.t[:, hd, c0:c0 + 128], ident_b.t[:], [ktT.buf, ident_b.buf], [auxb.buf], sig=(hd % 4 == 3))
            for hd in range(8):
                kk = ktok.next(); kts.append(kk)
                S.op("act", [auxb.buf], [kk.buf], lambda e: e.activation(out=kk.t[:, :], in_=auxb.t[:, hd * 128:(hd + 1) * 128], func=AF.Copy))
            for hd in range(8):
                a = aux.items[hd // 4]
                S.mm(a.t[:, (hd % 4) * 128:(hd % 4 + 1) * 128], ktT.t[:, hd, c0:c0 + 128], qtT.t[:, hd, c0:c0 + 128], True, True, [ktT.buf, qtT.buf], [a.buf], sig=(hd % 4 == 3))
            for hd in range(8):
                a = aux.items[hd // 4]
                at = AT.next(); ats.append(at)
                S.op("dve", [a.buf, cst.buf], [at.buf], lambda e: e.tensor_tensor(out=at.t[:, :], in0=a.t[:, (hd % 4) * 128:(hd % 4 + 1) * 128], in1=bdmask, op=ALU.mult))
            for hd in range(8):
                kk = kts[hd]; at = ats[hd]
                o_ = oac.t[:, hd * 128:(hd + 1) * 128]
                S.mm(o_, vb.t[:, pr, hd * 128:(hd + 1) * 128], at.t[:, :], hd % 4 == 0, False, [vb.buf, at.buf], [oac.buf], sig=False, skip=True)
                S.mm(oac.t[:, hd * 128:hd * 128 + 64], S_bf.t[:, hd, :], qtT.t[:, hd, c0:c0 + 64], False, False, [S_bf.buf, qtT.buf], [oac.buf], sig=False, skip=True)
                S.mm(su.t[:, hd * 128:(hd + 1) * 128], kk.t[0:64, :], vb.t[0:64, pr, hd * 128:(hd + 1) * 128], True, True, [kk.buf, vb.buf], [su.buf], sig=(hd == 7))
            S.op("dve", [su.buf, S_f.buf], [stmp.buf], lambda e: e.tensor_tensor(out=stmp.t[:, :, :], in0=su.t[:, :].rearrange("p (h e) -> p h e", h=8), in1=S_f.t[:, :, :], op=ALU.add))
            for hd in range(8):
                S.op("dve", [stmp.buf, elast.buf], [S_f.buf], lambda e: e.tensor_scalar(out=S_f.t[:, hd, :], in0=stmp.t[:, hd, :], scalar1=elast.t[:, hd, 2 * pr:2 * pr + 1], scalar2=None, op0=ALU.mult))
                S.op("act", [stmp.buf, elast.buf], [S_bf.buf], lambda e: e.activation(out=S_bf.t[:, hd, :], in_=stmp.t[:, hd, :], func=AF.Copy, scale=elast.t[:, hd, 2 * pr:2 * pr + 1]))
            su2 = su
            for hd in range(8):
                kk = kts[hd]
                S.mm(oac.t[:, hd * 128 + 64:hd * 128 + 128], S_bf.t[:, hd, :], qtT.t[:, hd, c0 + 64:c0 + 128], False, True, [S_bf.buf, qtT.buf], [oac.buf], sig=(hd == 7), skip=True)
                S.mm(su2.t[:, hd * 128:(hd + 1) * 128], kk.t[64:128, :], vb.t[64:128, pr, hd * 128:(hd + 1) * 128], True, True, [kk.buf, vb.buf], [su2.buf], sig=(hd == 7))
            S.op("dve", [su2.buf, S_f.buf], [stmp.buf], lambda e: e.tensor_tensor(out=stmp.t[:, :, :], in0=su2.t[:, :].rearrange("p (h e) -> p h e", h=8), in1=S_f.t[:, :, :], op=ALU.add))
            for hd in range(8):
                S.op("dve", [stmp.buf, elast.buf], [S_f.buf], lambda e: e.tensor_scalar(out=S_f.t[:, hd, :], in0=stmp.t[:, hd, :], scalar1=elast.t[:, hd, 2 * pr + 1:2 * pr + 2], scalar2=None, op0=ALU.mult))
                S.op("act", [stmp.buf, elast.buf], [S_bf.buf], lambda e: e.activation(out=S_bf.t[:, hd, :], in_=stmp.t[:, hd, :], func=AF.Copy, scale=elast.t[:, hd, 2 * pr + 1:2 * pr + 2]))
            o_sb = ol.next()
            S.op("act", [oac.buf], [o_sb.buf], lambda e: e.activation(out=o_sb.t[:, :, :], in_=oac.t[:, :].rearrange("p (h t) -> p h t", h=8), func=AF.Copy))
            S.dma("sp", obs[:, :, blk * TB + c0:blk * TB + c0 + 128], o_sb.t[:, :, :], [o_sb.buf], [obs_b[blk]], o_sb.buf)
        sc.close()

    def exchange():
        S.dma("sp", st_loc[:, :], S_f.t[:, :, :].rearrange("p h e -> p (h e)"), [S_f.buf], [st_loc_b], S_f.buf)
        S.deps("pool", [st_loc_b], [st_all_b])
        ccsem = nc.alloc_semaphore("ccsem")
        nc.gpsimd.collective_compute("AllGather", ALU.bypass, replica_groups=[[2 * i, 2 * i + 1] for i in range(ncores // 2)],
                                     ins=[st_loc], outs=[st_all]).then_inc(ccsem)
        tok = ("d", ccsem, 1)
        st_all_b.w["cc"] = tok
        st_loc_b.r["cc"] = tok
        sc = Scope()
        sr = Ring([sc.sb("sr%d" % i, [128, 1024], F32) for i in range(2)])
        S.op("dve", [], [S_f.buf], lambda e: e.memset(S_f.t[:], 0.0))
        Sf2 = S_f.t[:, :, :].rearrange("p h e -> p (h e)")
        for r in range(2):
            t = sr.next()
            S.dma("sp", t.t[:, :], st_all[r * 128:(r + 1) * 128, :], [st_all_b], [t.buf], t.buf)
            S.op("dve", [t.buf, S_f.buf, sml.buf], [S_f.buf], lambda e: e.scalar_tensor_tensor(out=Sf2, in0=t.t[:, :], scalar=sel[:, r:r + 1], in1=Sf2, op0=ALU.mult, op1=ALU.add))
        S.op("dve", [S_f.buf], [S_bf.buf], lambda e: e.tensor_copy(S_bf.t[:], S_f.t[:]))
        if mode.startswith("dbg"):
            dump("S_in", S_f)
        sc.close()

    def phase_b(blk):
        n = TB
        cols = slice(blk * TB, (blk + 1) * TB)
        sc0 = Scope()
        xT = sc0.sb("xT", [128, 16, TB], F32)
        S.dma("sp", xT.t[:, :, :], x1s[:, :, cols], [x1s_b[blk]], [xT.buf], xT.buf)
        rmsnorm_T(xT, n, 1, hT)
        rh = (lambda kc, c0, c1: hT.t[:, kc, c0:c1])
        sc1 = Scope()
        outbT = sc1.sb("outbT", [128, 8, TB], BF16)
        sc2 = Scope()
        ob = sc2.sb("ob", [128, 8, TB], F32)
        qtot = sc2.sb("qtot", [128, 8, TB], BF16)
        S.dma("sp", ob.t[:, :, :], obs[:, :, cols], [obs_b[blk]], [ob.buf], ob.buf)
        S.dma("sp", qtot.t[:, :, :], qts[:, :, cols], [qts_b[blk]], [qtot.buf], qtot.buf)
        for hd in range(8):
            a = accs.next()
            S.mm(a.t[:, 0:TB], S_bf.t[:, hd, :], qtot.t[:, hd, :], True, True, [S_bf.buf, qtot.buf], [a.buf], sig=True)
            S.op("dve", [a.buf, ob.buf], [ob.buf], lambda e: e.tensor_tensor(out=ob.t[:, hd, :], in0=a.t[:, 0:TB], in1=ob.t[:, hd, :], op=ALU.add))
            s = sq.next()
            S.op("act", [ob.buf], [s.buf], lambda e: e.activation(out=s.t[:, :TB], in_=ob.t[:, hd, :], func=AF.Square))
            a2 = accs.next()
            S.mm(a2.t[:, 0:TB], ones_b.t[:], s.t[:, 0:TB], True, True, [ones_b.buf, s.buf], [a2.buf], sig=True)
            S.op("act", [a2.buf, cc.buf], [rstd.buf], lambda e: e.activation(out=rstd.t[:, :TB], in_=a2.t[:, :TB], func=AF.Sqrt, scale=1.0 / 128, bias=epsc))
            S.op("dve", [rstd.buf], [rstd.buf], lambda e: e.reciprocal(rstd.t[:, :TB], rstd.t[:, :TB]))
            S.op("dve", [ob.buf, rstd.buf, sml.buf], [ob.buf], lambda e: e.scalar_tensor_tensor(out=ob.t[:, hd, :], in0=ob.t[:, hd, :], scalar=hgw[:, hd:hd + 1], in1=rstd.t[:, :TB], op0=ALU.mult, op1=ALU.mult))

        def epi_og(wi, fc, a):
            s = sg.next()
            S.op("act", [a.buf], [s.buf], lambda e: e.activation(out=s.t[:, :TB], in_=a.t[:, :TB], func=AF.Silu))
            S.op("dve", [ob.buf, s.buf], [outbT.buf], lambda e: e.tensor_tensor(out=outbT.t[:, fc, :], in0=ob.t[:, fc, :], in1=s.t[:, :TB], op=ALU.mult))

        gemm([(plain(w_in, W_OFF["og"], D, 512), D, rh, [hT.buf])], 2, 512, n, epi_og)
        if mode.startswith("dbg") and blk == 0:
            dump("outbT", outbT)
        sc2.close()
        sc3 = Scope()
        oaT = sc3.sb("oaTb", [128, 8, TB], BF16)
        mT = sc3.sb("mT", [128, 16, TB], BF16)
        tA = Ring([sc3.sb("tA%d" % i, [128, TB], F32) for i in range(2)])
        tM = Ring([sc3.sb("tM%d" % i, [128, TB], F32) for i in range(2)])
        S.dma("sp", oaT.t[:, :, :], oas[:, :, cols], [oas_b[blk]], [oaT.buf], oaT.buf)
        cur = {}

        def epi_m(wi, fc, a):
            if wi == 0:
                t = tA.next(); cur["a"] = t
                S.op("act", [a.buf], [t.buf], lambda e: e.activation(out=t.t[:, :], in_=a.t[:, :TB], func=AF.Sigmoid))
            elif wi == 1:
                t = tM.next(); cur["m"] = t
                S.op("dve", [a.buf, cur["a"].buf], [t.buf], lambda e: e.tensor_tensor(out=t.t[:, :], in0=a.t[:, :TB], in1=cur["a"].t[:, :], op=ALU.mult))
            elif wi == 2:
                t = tA.next(); cur["b"] = t
                S.op("act", [a.buf], [t.buf], lambda e: e.activation(out=t.t[:, :], in_=a.t[:, :TB], func=AF.Sigmoid))
            else:
                t = cur["b"]
                S.op("dve", [a.buf, t.buf], [t.buf], lambda e: e.tensor_tensor(out=t.t[:, :], in0=a.t[:, :TB], in1=t.t[:, :], op=ALU.mult))
                S.op("dve", [t.buf, cur["m"].buf], [mT.buf], lambda e: e.tensor_tensor(out=mT.t[:, fc, :], in0=t.t[:, :], in1=cur["m"].t[:, :], op=ALU.add))

        gemm([(plain(w_in, W_OFF["ga"], D, 256), D, rh, [hT.buf]),
              (plain(W["w_up_a"], 0, 1024, 256), 1024, lambda kc, c0, c1: oaT.t[:, kc, c0:c1], [oaT.buf]),
              (plain(w_in, W_OFF["gb"], D, 256), D, rh, [hT.buf]),
              (plain(W["w_up_b"], 0, 1024, 256), 1024, lambda kc, c0, c1: outbT.t[:, kc, c0:c1], [outbT.buf])],
             D // 256, 256, n, epi_m)

        def epi_o(wi, fc, a):
            S.op("dve", [a.buf, xT.buf], [xT.buf], lambda e: e.tensor_tensor(out=xT.t[:, fc, :], in0=a.t[:, :TB], in1=xT.t[:, fc, :], op=ALU.add))

        gemm([(plain(W["w_out"], 0, D, 512), D, lambda kc, c0, c1: mT.t[:, kc, c0:c1], [mT.buf])], D // 512, 512, n, epi_o)
        if mode.startswith("dbg") and blk == 0:
            dump("x2T", xT)
        sc3.close()
        sc1.close()
        sc4 = Scope()
        actT = sc4.sb("actT2", [128, FH // 128, TB], BF16)
        ffn(xT, actT, n, 2, W["ffn2_w_gate"], W["ffn2_w_up"], W["ffn2_w_down"])
        sc4.close()
        sc5 = Scope()
        ptok = sc5.sb("ptok", [128, PLE], F32)
        pT = sc5.sb("pT", [128, 2, TB], BF16)
        for t in range(4):
            r0 = blk * TB + t * 128
            transpose_in(p_in[r0:r0 + 128, :], 2, ptok, pT, t * 128)
        rmsnorm_T(xT, n, 3, hT)
        cur2 = {}

        def epi_p(wi, fc, a):
            if wi == 0:
                s = sg.next(); cur2["s"] = s
                S.op("act", [a.buf], [s.buf], lambda e: e.activation(out=s.t[:, :TB], in_=a.t[:, :TB], func=AF.Sigmoid))
            else:
                s = cur2["s"]
                S.op("dve", [a.buf, s.buf], [s.buf], lambda e: e.tensor_tensor(out=s.t[:, :TB], in0=a.t[:, :TB], in1=s.t[:, :TB], op=ALU.mult))
                S.op("dve", [s.buf, xT.buf], [xT.buf], lambda e: e.tensor_tensor(out=xT.t[:, fc, :], in0=s.t[:, :TB], in1=xT.t[:, fc, :], op=ALU.add))

        gemm([(plain(W["ple_w_gate"], 0, D, 512), D, rh, [hT.buf]),
              (plain(W["ple_w_proj"], 0, PLE, 512), PLE, lambda kc, c0, c1: pT.t[:, kc, c0:c1], [pT.buf])],
             D // 512, 512, n, epi_p)
        sc5.close()
        rmsnorm_T(xT, n, 4, xT)
        sc6 = Scope()
        xt = sc6.sb("xtok_o", [128, D], F32)
        store_out(blk, xT, xt)
        sc6.close()
        sc0.close()

    nb = NBLK
    for blk in range(nb):
        phase_a(blk)
    exchange()
    for blk in range(nb):
        phase_b(blk)

    S.wait_all("sp", [ybuf])
    for b in dbg_bufs:
        S.wait_all("pool", [b])
    return nc


def _consts(first_of_seq):
    c = np.zeros((128, CW), np.float32)
    c[:, 0:128] = np.eye(128, dtype=np.float32)
    q = np.arange(128)[:, None]
    k = np.arange(256)[None, :]
    dist = q + 128 - k
    allowed = (dist >= 0) & (dist < 128)
    c[:, 128:384] = np.where(allowed, 1.0, 0.0)
    allowedA = allowed & (k >= 128) if first_of_seq else allowed
    c[:, 384:640] = np.where(allowedA, 1.0, 0.0)
    for m in range(128):
        dm = m % 64
        if dm < 8:
            c[m + 8, 640 + m] = -1.0
        elif dm < 16:
            c[m - 8, 640 + m] = 1.0
    t = np.arange(512)
    c[:, 768:1280] = (t % 64 != 0).astype(np.float32)[None, :]
    s = np.arange(128)[:, None]
    tt = np.arange(128)[None, :]
    c[:, 1280:1408] = ((s // 64 == tt // 64) & (s <= tt)).astype(np.float32)
    return c


def _norm_pack(*vs):
    out = np.zeros((128, 6, 16), np.float32)
    for i, v in enumerate(vs):
        out[:, i, :] = np.asarray(v, np.float32).reshape(16, 128).T
    return out


def _small(hgrn_lower_bound, hgrn_norm, attn_sinks, core, partner):
    s = np.zeros((128, 64), np.float32)
    lbp = np.asarray(hgrn_lower_bound, np.float32)
    s[:, 0:8] = lbp[0].reshape(8, 128).T
    s[:, 8:16] = lbp[1].reshape(8, 128).T
    s[:, 16:24] = np.asarray(hgrn_norm, np.float32).reshape(8, 128).T
    s[:, 24:40] = np.asarray(attn_sinks, np.float32).reshape(1, 16)
    if partner is not None:
        s[:, 40 + partner] = 1.0
    inv = np.power(np.float32(500000.0), -np.arange(0, 16, 2, dtype=np.float32) / np.float32(16)).astype(np.float32)
    for m in range(128):
        dm = m % 64
        if dm < 16:
            s[m, 48] = inv[dm % 8]
    return s


def make_in_maps(inputs, ncores=NCORES):
    x = np.asarray(inputs["x"], np.float32)
    B, SEQ, _ = x.shape
    xf = x.reshape(B * SEQ, D)
    pf = np.asarray(inputs["p"], np.float32).reshape(B * SEQ, PLE)
    posf = np.asarray(inputs["positions"]).astype(np.int32).reshape(B * SEQ)
    norms = _norm_pack(inputs["ffn1_norm"][0], inputs["mix_norm"][0], inputs["ffn2_norm"][0], inputs["ple_norm"][0], inputs["final_norm"])
    wts = {k: np.ascontiguousarray(np.asarray(inputs[k], np.float32)[0]) for k in
           ("ffn1_w_gate", "ffn1_w_up", "ffn1_w_down", "w_in", "w_up_a", "w_up_b", "w_out",
            "ffn2_w_gate", "ffn2_w_up", "ffn2_w_down", "ple_w_gate", "ple_w_proj")}
    in_maps = []
    for c in range(ncores):
        r0 = c * TOK
        first = (r0 % SEQ) == 0
        xc = np.zeros((TOK + LB, D), np.float32)
        xc[:TOK] = xf[r0:r0 + TOK]
        pc = np.zeros((1, TOK + LB), np.int32)
        pc[0, :TOK] = posf[r0:r0 + TOK]
        if not first:
            xc[TOK:] = xf[r0 - LB:r0]
            pc[0, TOK:] = posf[r0 - LB:r0]
        m = {"x": xc, "p": np.ascontiguousarray(pf[r0:r0 + TOK]), "pos": pc, "consts": _consts(first), "norms": norms,
             "small": _small(inputs["hgrn_lower_bound"], inputs["hgrn_norm"][0], inputs["attn_sinks"][0], c, None if first else (c - 1) % 2)}
        m.update(wts)
        in_maps.append(m)
    return in_maps, (B, SEQ)


def kernel(**inputs):
    in_maps, (B, SEQ) = make_in_maps(inputs)
    nc = build("full")
    res = run_bass_kernel_spmd(nc, in_maps, core_ids=list(range(NCORES)))
    y = np.concatenate([r["y"] for r in res.results], axis=0)
    return y.reshape(B, SEQ, D).astype(np.float32)
```
